# Optimizing a Trainium2 kernel written in Bass

```python
import jax, jax.numpy as jnp
from jax import lax
import numpy as np

D_MODEL = 1024
BATCH = 32
SEQ = 256
DEPTH = 4
DEC_BATCH = 2
DEC_SEQ = 1024
PAST_LEN = 256

GRID_W = 64
H_A = 8
DK_A = 128
DV_A = 128
WA = H_A * DK_A
H_B = 8
DH_B = 128
WB = H_B * DH_B
CHUNK_A = 16
CHUNK_B = 64
D_FF = 2816
N_EXPERTS = 8
TOP_K = 2
D_FF_E = 2816
N_DENSE = (DEPTH + 1) // 2
N_MOE = DEPTH // 2
ALPHA = (2 * DEPTH) ** 0.25
BETA = (8 * DEPTH) ** -0.25
IN_SIZES = (WA, WA, WA, WA, WA, WB, WB, WB, 4 * H_B, D_MODEL, D_MODEL)
N_IN = sum(IN_SIZES)

kernel_name = 'hgrn2_mlstm_bidir_diffusion_step'


def layer_norm(x, g, b, eps=1e-5):
    xf = x.astype(jnp.float32)
    mu = jnp.mean(xf, axis=-1, keepdims=True)
    var = jnp.mean(jnp.square(xf - mu), axis=-1, keepdims=True)
    return ((xf - mu) * lax.rsqrt(var + eps) * g + b).astype(x.dtype)


def rms_norm_heads(x, g, eps=1e-6):
    return x * lax.rsqrt(jnp.mean(jnp.square(x), axis=-1, keepdims=True) + eps) * g


def group_norm_heads(x, g, eps=1e-6):
    mu = jnp.mean(x, axis=-1, keepdims=True)
    var = jnp.mean(jnp.square(x - mu), axis=-1, keepdims=True)
    return (x - mu) * lax.rsqrt(var + eps) * g


def split_heads(t, h):
    return t.reshape(*t.shape[:-1], h, t.shape[-1] // h)


def chunk(x, L):
    B, T, H = x.shape[:3]
    rest = x.shape[3:]
    x = x.reshape(B, T // L, L, H, *rest)
    return jnp.swapaxes(jnp.moveaxis(x, 1, 0), 2, 3)


def unchunk(y):
    Nc, B, H, L = y.shape[:4]
    rest = y.shape[4:]
    y = jnp.moveaxis(jnp.swapaxes(y, 2, 3), 0, 1)
    return y.reshape(B, Nc * L, H, *rest)


def hgrn2_scan(q, k, v, log_f, S0):
    L = CHUNK_A
    mask = jnp.tril(jnp.ones((L, L), dtype=bool))

    def step(S, inp):
        qc, kc, vc, gc = inp
        b = jnp.cumsum(gc, axis=2)
        diff = b[:, :, :, None, :] - b[:, :, None, :, :]
        decay = jnp.exp(jnp.where(mask[:, :, None], diff, -jnp.inf))
        att = jnp.einsum('bhtk,bhtsk,bhsk->bhts', qc, decay, kc)
        o = (jnp.einsum('bhtk,bhkv->bhtv', qc * jnp.exp(b), S)
             + jnp.einsum('bhts,bhsv->bhtv', att, vc))
        b_last = b[:, :, -1:, :]
        S_new = (jnp.exp(b_last[:, :, 0])[..., None] * S
                 + jnp.einsum('bhsk,bhsv->bhkv', kc * jnp.exp(b_last - b), vc))
        return S_new, o

    S_T, o = lax.scan(step, S0, (chunk(q, L), chunk(k, L), chunk(v, L), chunk(log_f, L)))
    return unchunk(o), S_T


def mlstm_scan(q, k, v, ig, lf, C0, n0, m0):
    L = CHUNK_B
    mask = jnp.tril(jnp.ones((L, L), dtype=bool))

    def step(carry, inp):
        C, n, m = carry
        qc, kc, vc, ic, fc = inp
        F = jnp.cumsum(fc, axis=-1)
        logd = jnp.where(mask, F[..., :, None] - F[..., None, :] + ic[..., None, :], -jnp.inf)
        log_inter = F + m[..., None]
        m_q = jnp.maximum(log_inter, jnp.max(logd, axis=-1))
        w = jnp.exp(logd - m_q[..., None])
        a = jnp.exp(log_inter - m_q)
        s = jnp.einsum('bhtd,bhsd->bhts', qc, kc) * w
        num = (a[..., None] * jnp.einsum('bhtd,bhde->bhte', qc, C)
               + jnp.einsum('bhts,bhse->bhte', s, vc))
        den = a * jnp.einsum('bhtd,bhd->bht', qc, n) + jnp.sum(s, axis=-1)
        h = num / jnp.maximum(jnp.abs(den), jnp.exp(-m_q))[..., None]
        F_last = F[..., -1]
        log_w = F_last[..., None] - F + ic
        m_new = jnp.maximum(F_last + m, jnp.max(log_w, axis=-1))
        ws = jnp.exp(log_w - m_new[..., None])
        d0 = jnp.exp(F_last + m - m_new)
        C_new = d0[..., None, None] * C + jnp.einsum('bhs,bhsd,bhse->bhde', ws, kc, vc)
        n_new = d0[..., None] * n + jnp.einsum('bhs,bhsd->bhd', ws, kc)
        return (C_new, n_new, m_new), h

    (C_T, n_T, m_T), h = lax.scan(step, (C0, n0, m0),
                                  (chunk(q, L), chunk(k, L), chunk(v, L), chunk(ig, L), chunk(lf, L)))
    return unchunk(h), (C_T, n_T, m_T)


def depthwise_conv(img, w, b):
    rhs = w[:, :, None, :].astype(img.dtype)
    y = lax.conv_general_dilated(img, rhs, window_strides=(1, 1), padding='SAME',
                                 dimension_numbers=('NHWC', 'HWIO', 'NHWC'),
                                 feature_group_count=img.shape[-1])
    return y + b


def mixer(h, rows, s_a, s_C, s_n, s_m, lb, w_in, b_in, a_norm_g, conv_w, conv_b,
          w_mq, w_mk, f_bias, b_norm_g, w_ba, w_bb, w_out):
    B, T, _ = h.shape
    f32 = jnp.float32
    proj = jnp.einsum('btd,dn->btn', h, w_in) + b_in
    (a_q, a_f_fwd, a_f_bwd, a_i, a_g, b_qk, b_v, b_o, b_gates, g_ma, g_mb) = jnp.split(
        proj, np.cumsum(IN_SIZES)[:-1].tolist(), axis=-1)

    def bidir(t_fwd, t_bwd):
        return jnp.stack([t_fwd, jnp.flip(t_bwd, axis=1)])

    z = bidir(a_f_fwd, a_f_bwd).astype(f32)
    lbb = lb.astype(f32)[:, None, None, :]
    log_f = jnp.logaddexp(jnp.log(lbb), jnp.log1p(-lbb) + jax.nn.log_sigmoid(z))
    k_a = (1.0 - lbb) * jax.nn.sigmoid(-z)
    q_a = split_heads(a_q.astype(f32), H_A)
    v_a = split_heads(a_i.astype(f32), H_A)
    o_a, sa_T = jax.vmap(hgrn2_scan)(bidir(q_a, q_a), split_heads(k_a, H_A), bidir(v_a, v_a),
                                     split_heads(log_f, H_A), jnp.moveaxis(s_a.astype(f32), 1, 0))
    o_a = o_a[0] + jnp.flip(o_a[1], axis=1)
    y_a = (rms_norm_heads(o_a, a_norm_g) * jax.nn.silu(split_heads(a_g.astype(f32), H_A))).reshape(B, T, WA)

    if rows is None:
        img, w_k = b_qk[:, None], conv_w[1:2]
    else:
        img, w_k = b_qk.reshape(B, rows, GRID_W, WB), conv_w
    conv_act = jax.nn.silu(depthwise_conv(img, w_k, conv_b).reshape(B, T, WB))
    ca = split_heads(conv_act, H_B)
    q_b = jnp.einsum('bthd,hde->bthe', ca, w_mq).astype(f32)
    k_b = jnp.einsum('bthd,hde->bthe', ca, w_mk).astype(f32) * (DH_B ** -0.5)
    v_b = split_heads(b_v.astype(f32), H_B)
    gts = b_gates.astype(f32).reshape(B, T, 4, H_B)
    fb = f_bias.astype(f32)
    i_g = bidir(gts[:, :, 0], gts[:, :, 2])
    log_fg = jax.nn.log_sigmoid(bidir(gts[:, :, 1] + fb[0], gts[:, :, 3] + fb[1]))
    h_b, (C_T, n_T, m_T) = jax.vmap(mlstm_scan)(
        bidir(q_b, q_b), bidir(k_b, k_b), bidir(v_b, v_b), i_g, log_fg,
        jnp.moveaxis(s_C.astype(f32), 1, 0), jnp.moveaxis(s_n.astype(f32), 1, 0),
        jnp.moveaxis(s_m.astype(f32), 1, 0))
    h_b = h_b[0] + jnp.flip(h_b[1], axis=1)
    y_b = (group_norm_heads(h_b, b_norm_g) * jax.nn.sigmoid(split_heads(b_o.astype(f32), H_B))).reshape(B, T, WB)

    merged = (jax.nn.sigmoid(g_ma.astype(f32)) * jnp.einsum('btc,cd->btd', y_a, w_ba)
              + jax.nn.sigmoid(g_mb.astype(f32)) * jnp.einsum('btc,cd->btd', y_b, w_bb))
    y = jnp.einsum('btd,de->bte', merged, w_out)
    new_states = (jnp.moveaxis(sa_T, 0, 1), jnp.moveaxis(C_T, 0, 1),
                  jnp.moveaxis(n_T, 0, 1), jnp.moveaxis(m_T, 0, 1))
    return y.astype(h.dtype), new_states


def swiglu(h, w1, w3, w2):
    return jnp.einsum('btf,fd->btd', jax.nn.silu(jnp.einsum('btd,df->btf', h, w1))
                      * jnp.einsum('btd,df->btf', h, w3), w2)


def moe_swiglu(h, wr, br, w1, w3, w2):
    B, T, D = h.shape
    t = h.reshape(B * T, D)
    logits = jnp.matmul(t, wr).astype(jnp.float32) + br
    probs = jax.nn.softmax(logits, axis=-1)
    top_p, top_i = lax.top_k(probs, TOP_K)
    top_p = top_p / jnp.sum(top_p, axis=-1, keepdims=True)
    comb = jnp.sum(jax.nn.one_hot(top_i, N_EXPERTS, dtype=jnp.float32) * top_p[..., None], axis=1)
    out = jnp.zeros((B * T, D), jnp.float32)
    for e in range(N_EXPERTS):
        ye = jnp.matmul(jax.nn.silu(jnp.matmul(t, w1[e])) * jnp.matmul(t, w3[e]), w2[e])
        out = out + comb[:, e:e + 1] * ye
    return out.reshape(B, T, D).astype(h.dtype)


def trunk(x, cond, rows, s_a, s_C, s_n, s_m, lb, w_ada, b_ada, w_in, b_in, hgrn_norm_g,
          conv_w, conv_b, w_mq, w_mk, mlstm_fbias, mlstm_norm_g, w_branch_a, w_branch_b, w_out,
          ln1_g, ln1_b, ln2_g, ln2_b, ffn_w1, ffn_w3, ffn_w2,
          moe_router_w, moe_router_b, moe_w1, moe_w3, moe_w2):
    out_a, out_C, out_n, out_m = [], [], [], []
    act = jax.nn.silu(cond)
    for l in range(DEPTH):
        mod = jnp.einsum('bd,dn->bn', act, w_ada[l]) + b_ada[l]
        sh1, sc1, g1, sh2, sc2, g2 = jnp.split(mod[:, None, :], 6, axis=-1)
        y, (sa, sC, sn, sm) = mixer(x * (1 + sc1) + sh1, rows, s_a[:, l], s_C[:, l], s_n[:, l], s_m[:, l],
                                    lb[:, l], w_in[l], b_in[l], hgrn_norm_g[l], conv_w[l], conv_b[l],
                                    w_mq[l], w_mk[l], mlstm_fbias[l], mlstm_norm_g[l],
                                    w_branch_a[l], w_branch_b[l], w_out[l])
        x = layer_norm(ALPHA * x + g1 * y, ln1_g[l], ln1_b[l])
        hh = x * (1 + sc2) + sh2
        j = l // 2
        if l % 2 == 0:
            f = swiglu(hh, ffn_w1[j], ffn_w3[j], ffn_w2[j])
        else:
            f = moe_swiglu(hh, moe_router_w[j], moe_router_b[j], moe_w1[j], moe_w3[j], moe_w2[j])
        x = layer_norm(ALPHA * x + g2 * f, ln2_g[l], ln2_b[l])
        out_a.append(sa)
        out_C.append(sC)
        out_n.append(sn)
        out_m.append(sm)
    return (x, jnp.stack(out_a, axis=1), jnp.stack(out_C, axis=1),
            jnp.stack(out_n, axis=1), jnp.stack(out_m, axis=1))


def setup_inputs(seed: int = 0) -> dict:
    key = jax.random.key(seed)
    ks = iter(jax.random.split(key, 48))
    f32 = jnp.float32

    def nrm(shape, s):
        return jax.random.normal(next(ks), shape, f32) * s

    D = D_MODEL
    return {
        'x_prompt': nrm((BATCH, SEQ, D), 1.0),
        'x_sample': nrm((DEC_BATCH, DEC_SEQ, D), 1.0),
        'c': nrm((DEC_BATCH, D), 1.0),
        'state_hgrn': nrm((DEC_BATCH, DEPTH, 2, H_A, DK_A, DV_A), 1.0),
        'state_mlstm_C': nrm((DEC_BATCH, DEPTH, 2, H_B, DH_B, DH_B), 0.2),
        'state_mlstm_n': nrm((DEC_BATCH, DEPTH, 2, H_B, DH_B), 0.2),
        'state_mlstm_m': nrm((DEC_BATCH, DEPTH, 2, H_B), 1.0),
        'c_ctx': nrm((D,), 1.0),
        'w_ada': nrm((DEPTH, D, 6 * D), 0.5 * D ** -0.5),
        'b_ada': nrm((DEPTH, 6 * D), 0.02),
        'w_in': nrm((DEPTH, D, N_IN), D ** -0.5),
        'b_in': nrm((DEPTH, N_IN), 0.02),
        'hgrn_lb_raw': nrm((2, DEPTH, WA), 0.5),
        'hgrn_norm_g': 1.0 + nrm((DEPTH, DV_A), 0.02),
        'conv_w': nrm((DEPTH, 3, 3, WB), 9 ** -0.5),
        'conv_b': nrm((DEPTH, WB), 0.02),
        'w_mq': nrm((DEPTH, H_B, DH_B, DH_B), DH_B ** -0.5),
        'w_mk': nrm((DEPTH, H_B, DH_B, DH_B), DH_B ** -0.5),
        'mlstm_fbias': jnp.linspace(3.0, 6.0, H_B, dtype=f32) + nrm((DEPTH, 2, H_B), 0.1),
        'mlstm_norm_g': 1.0 + nrm((DEPTH, DH_B), 0.02),
        'w_branch_a': nrm((DEPTH, WA, D), BETA * WA ** -0.5),
        'w_branch_b': nrm((DEPTH, WB, D), BETA * WB ** -0.5),
        'w_out': nrm((DEPTH, D, D), BETA * D ** -0.5),
        'ln1_g': 1.0 + nrm((DEPTH, D), 0.02),
        'ln1_b': nrm((DEPTH, D), 0.02),
        'ln2_g': 1.0 + nrm((DEPTH, D), 0.02),
        'ln2_b': nrm((DEPTH, D), 0.02),
        'ffn_w1': nrm((N_DENSE, D, D_FF), BETA * D ** -0.5),
        'ffn_w3': nrm((N_DENSE, D, D_FF), BETA * D ** -0.5),
        'ffn_w2': nrm((N_DENSE, D_FF, D), BETA * D_FF ** -0.5),
        'moe_router_w': nrm((N_MOE, D, N_EXPERTS), D ** -0.5),
        'moe_router_b': nrm((N_MOE, N_EXPERTS), 0.01),
        'moe_w1': nrm((N_MOE, N_EXPERTS, D, D_FF_E), BETA * D ** -0.5),
        'moe_w3': nrm((N_MOE, N_EXPERTS, D, D_FF_E), BETA * D ** -0.5),
        'moe_w2': nrm((N_MOE, N_EXPERTS, D_FF_E, D), BETA * D_FF_E ** -0.5),
    }


def reference(x_prompt, x_sample, c, state_hgrn, state_mlstm_C, state_mlstm_n, state_mlstm_m, c_ctx,
              w_ada, b_ada, w_in, b_in, hgrn_lb_raw, hgrn_norm_g, conv_w, conv_b, w_mq, w_mk,
              mlstm_fbias, mlstm_norm_g, w_branch_a, w_branch_b, w_out, ln1_g, ln1_b, ln2_g, ln2_b,
              ffn_w1, ffn_w3, ffn_w2, moe_router_w, moe_router_b, moe_w1, moe_w3, moe_w2):
    f32 = jnp.float32
    lb = jnp.cumsum(jax.nn.softmax(hgrn_lb_raw.astype(f32), axis=1), axis=1)
    lb = lb - lb[:, :1]
    weights = (lb, w_ada, b_ada, w_in, b_in, hgrn_norm_g, conv_w, conv_b, w_mq, w_mk,
               mlstm_fbias, mlstm_norm_g, w_branch_a, w_branch_b, w_out, ln1_g, ln1_b, ln2_g, ln2_b,
               ffn_w1, ffn_w3, ffn_w2, moe_router_w, moe_router_b, moe_w1, moe_w3, moe_w2)

    nb = x_prompt.shape[0]
    z_a = jnp.zeros((nb, DEPTH, 2, H_A, DK_A, DV_A), f32)
    z_C = jnp.zeros((nb, DEPTH, 2, H_B, DH_B, DH_B), f32)
    z_n = jnp.zeros((nb, DEPTH, 2, H_B, DH_B), f32)
    z_m = jnp.zeros((nb, DEPTH, 2, H_B), f32)
    cond_ctx = jnp.broadcast_to(c_ctx, (nb, c_ctx.shape[-1]))
    y_prompt, new_hgrn, new_C, new_n, new_m = trunk(x_prompt, cond_ctx, None, z_a, z_C, z_n, z_m, *weights)

    rows = x_sample.shape[1] // GRID_W
    y_sample, _, _, _, _ = trunk(x_sample, c, rows, state_hgrn, state_mlstm_C, state_mlstm_n,
                                 state_mlstm_m, *weights)
    return (y_prompt, y_sample, new_hgrn, new_C, new_n, new_m)
```

```python
import contextlib
import numpy as np
import concourse.bass as bass
import concourse.mybir as mybir
from concourse.bass_utils import run_bass_kernel_spmd

F32 = mybir.dt.float32
BF16 = mybir.dt.bfloat16
AF = mybir.ActivationFunctionType
ALU = mybir.AluOpType
AX = mybir.AxisListType

NCORES = 8
DEPTH = 4
D = 1024
NT = 1280
NTILE = 10
NSEG = 5
DFF = 2816
NF = 22
NE = 8
N_IN = 10272
ALPHA = (2 * DEPTH) ** 0.25
PIECES = [(0, 512), (512, 512), (1024, 256)]
FGROUPS = [(0, 4), (4, 8), (8, 12), (12, 16), (16, 19), (19, 22)]
PADA = 65


class Buf:
    __slots__ = ("name", "w", "r", "psum")

    def __init__(self, name, psum=False):
        self.name = name
        self.w = None
        self.r = []
        self.psum = psum


class Sched:
    EPOCH = 20000

    def __init__(self, nc):
        self.nc = nc
        self.eng = {"pe": nc.tensor, "dve": nc.vector, "act": nc.scalar, "pool": nc.gpsimd, "sp": nc.sync}
        self.prog = {k: [] for k in self.eng}
        self.cnt = {k: 0 for k in self.eng}
        self.epoch = {k: 0 for k in self.eng}
        self.seen = {k: {} for k in self.eng}
        self.semnames = []
        self.semobj = {}
        self.dmacnt = {}
        self.out_dma = []
        self.fence = []
        self.marks = []

    def mark(self, name):
        self.marks.append((name, {e: len(self.prog[e]) for e in self.eng}))

    def barrier(self):
        f = []
        for e in self.eng:
            if self.cnt[e] > 0:
                f.append((("E", e, self.epoch[e]), self.cnt[e]))
        for key, v in self.dmacnt.items():
            f.append((key, v))
        self.fence = f

    def _semkey(self, key):
        if key not in self.semobj:
            self.semobj[key] = None
            self.semnames.append(key)
        return key

    def _deps(self, e, reads, writes):
        deps = list(self.fence)
        for b in reads:
            if b.w is not None:
                deps.append(b.w)
            if b.psum:
                deps.extend(b.r)
        for b in writes:
            if b.w is not None:
                deps.append(b.w)
            deps.extend(b.r)
        need = {}
        seen = self.seen[e]
        for (k, v) in deps:
            if k[0] == "E" and k[1] == e and e == "pe":
                continue
            if seen.get(k, 0) >= v:
                continue
            if need.get(k, 0) < v:
                need[k] = v
        for k, v in need.items():
            seen[k] = v
        return list(need.items())

    def op(self, e, fn, reads=(), writes=()):
        waits = self._deps(e, reads, writes)
        if self.cnt[e] >= self.EPOCH:
            self.epoch[e] += 1
            self.cnt[e] = 0
        key = self._semkey(("E", e, self.epoch[e]))
        self.cnt[e] += 1
        tok = (key, self.cnt[e])
        self.prog[e].append((waits, fn, key, 1))
        for b in writes:
            b.w = tok
            b.r = []
        for b in reads:
            if b not in writes:
                if b.psum:
                    b.w = tok
                    b.r = []
                else:
                    b.r.append(tok)
        return tok

    def dma(self, q, out_ap, in_ap, reads=(), writes=(), semkey=None, is_out=False, **kw):
        waits = self._deps(q, reads, writes)
        key = self._semkey(("D", semkey))
        self.dmacnt[key] = self.dmacnt.get(key, 0) + 16
        tok = (key, self.dmacnt[key])

        kw = dict(kw)
        kw.setdefault("allow_slow_non_contiguous", True)

        def fn(eng, out_ap=out_ap, in_ap=in_ap, kw=kw):
            return eng.dma_start(out=out_ap, in_=in_ap, **kw)
        self.prog[q].append((waits, fn, key, 16))
        for b in writes:
            b.w = tok
            b.r = []
        for b in reads:
            b.r.append(tok)
        if is_out:
            self.out_dma.append(tok)
        return tok

    def emit(self):
        nc = self.nc
        fin = {}
        for (k, v) in self.out_dma:
            fin[k] = max(fin.get(k, 0), v)
        with contextlib.ExitStack() as st:
            for key in self.semnames:
                nm = "s_" + "_".join(str(x) for x in key)
                self.semobj[key] = st.enter_context(nc.semaphore(nm))
            so = self.semobj
            prog = self.prog
            for key in self.semnames:
                nc.gpsimd.sem_clear(so[key])
            nc.all_engine_barrier()

            def run(e, eng):
                for (waits, fn, key, inc) in prog[e]:
                    for (k, v) in waits:
                        eng.wait_ge(so[k], v)
                    fn(eng).then_inc(so[key], inc)

            with nc.Block() as block:
                @block.tensor
                def _(eng):
                    run("pe", eng)

                @block.vector
                def _(eng):
                    run("dve", eng)

                @block.scalar
                def _(eng):
                    run("act", eng)

                @block.gpsimd
                def _(eng):
                    run("pool", eng)

                @block.sync
                def _(eng):
                    run("sp", eng)
                    for k, v in fin.items():
                        eng.wait_ge(so[k], v)
            nc.all_engine_barrier()
            for key in self.semnames:
                nc.gpsimd.sem_clear(so[key])
            nc.all_engine_barrier()


class Ctx:
    def __init__(self, nc):
        self.nc = nc
        self.S = Sched(nc)
        self.st = contextlib.ExitStack()
        self.nps = 0
        self.pbanks = []
        self.reserved = set()
        self.uid = 0

    def sb(self, name, shape, dt=F32):
        t = self.st.enter_context(self.nc.sbuf_tensor("t_" + name, shape, dt))
        return t

    def tile(self, name, shape, dt=F32):
        return self.sb(name, shape, dt), Buf(name)

    def init_psum(self):
        for i in range(8):
            t = self.st.enter_context(self.nc.psum_tensor(f"pb{i}", [128, 512], F32))
            self.pbanks.append((t, Buf(f"pb{i}", psum=True)))

    def ps(self):
        while True:
            i = self.nps % 8
            self.nps += 1
            if i not in self.reserved:
                return self.pbanks[i]

    def ps_reserve(self, n):
        out = []
        for _ in range(n):
            while True:
                i = self.nps % 8
                self.nps += 1
                if i not in self.reserved:
                    break
            self.reserved.add(i)
            out.append(self.pbanks[i])
        return out

    def ps_release(self):
        self.reserved = set()

    def mm(self, out, lhsT, rhs, start, stop, R, W):
        self.S.op("pe", lambda e: e.matmul(out, lhsT=lhsT, rhs=rhs, start=start, stop=stop), reads=R, writes=W)

    def tr(self, out, in_, ident, R, W):
        self.S.op("pe", lambda e: e.transpose(out, in_, ident), reads=R, writes=W)

    def act(self, out, in_, func, R, W, bias=None, scale=1.0):
        if bias is None:
            self.S.op("act", lambda e: e.activation(out=out, in_=in_, func=func, scale=scale), reads=R, writes=W)
        else:
            self.S.op("act", lambda e: e.activation(out=out, in_=in_, func=func, bias=bias, scale=scale), reads=R, writes=W)

    def tt(self, eng, out, in0, in1, op, R, W):
        self.S.op(eng, lambda e: e.tensor_tensor(out=out, in0=in0, in1=in1, op=op), reads=R, writes=W)

    def ts(self, eng, out, in0, s1, s2, op0, op1, R, W):
        if s2 is None:
            self.S.op(eng, lambda e: e.tensor_scalar(out=out, in0=in0, scalar1=s1, scalar2=None, op0=op0), reads=R, writes=W)
        else:
            self.S.op(eng, lambda e: e.tensor_scalar(out=out, in0=in0, scalar1=s1, scalar2=s2, op0=op0, op1=op1), reads=R, writes=W)

    def stt(self, eng, out, in0, scalar, in1, op0, op1, R, W):
        self.S.op(eng, lambda e: e.scalar_tensor_tensor(out=out, in0=in0, scalar=scalar, in1=in1, op0=op0, op1=op1), reads=R, writes=W)

    def cp(self, eng, out, in_, R, W):
        if eng == "act":
            self.S.op("act", lambda e: e.copy(out=out, in_=in_), reads=R, writes=W)
        else:
            self.S.op(eng, lambda e: e.tensor_copy(out=out, in_=in_), reads=R, writes=W)

    def memset(self, eng, ap, val, W):
        self.S.op(eng, lambda e: e.memset(ap, val), writes=W)

    def dma(self, q, out, in_, R=(), W=(), sem=None, is_out=False, **kw):
        if sem is None:
            self.uid += 1
            sem = f"a{self.uid}_{q}"
        self.S.dma(q, out, in_, reads=R, writes=W, semkey=sem, is_out=is_out, **kw)


class Ring:
    def __init__(self, c, name, shape, n, dt=BF16, q="pool"):
        self.c = c
        self.slots = [c.tile(f"{name}{i}", shape, dt) for i in range(n)] if name != "w2_" else []
        self.n = n
        self.i = 0
        self.q = q
        self.name = name
        self.pending = []

    def load(self, fn):
        t, b = self.slots[self.i % self.n]
        sem = f"{self.name}{self.i % self.n}"
        self.i += 1
        pairs = fn(t)
        for (o, i_) in pairs:
            self.c.dma(self.q, o, i_, W=[b], sem=sem)
        return t, b


class Prefetch:
    def __init__(self, ring, fns, depth=None):
        self.ring = ring
        self.fns = fns
        self.depth = depth if depth is not None else max(1, ring.n - 1)
        self.loaded = []
        self.k = 0

    def _fill(self):
        while len(self.loaded) < len(self.fns) and len(self.loaded) < self.k + self.depth:
            self.loaded.append(self.ring.load(self.fns[len(self.loaded)]))

    def next(self):
        self._fill()
        r = self.loaded[self.k]
        self.k += 1
        return r


def build(nlayers=DEPTH, debug=False):
    nc = bass.Bass("TRN2", target_bir_lowering=False)
    c = Ctx(nc)
    S = c.S

    def din(name, shape):
        return nc.dram_tensor(name, shape, F32, kind="ExternalInput").ap()

    def dout(name, shape):
        return nc.dram_tensor(name, shape, F32, kind="ExternalOutput").ap()

    xin = din("xin", [NT, D])
    cond = din("cond", [NSEG, D])
    flag_d = din("flag", [128, 1])
    cmask_d = din("cmask", [2, 128, 1154])
    hinit = din("hinit", [DEPTH, 2, 8, 128, 128])
    cinit = din("cinit", [DEPTH, 2, 8, 128, 128])
    ninit = din("ninit", [DEPTH, 2, 8, 128])
    minit = din("minit", [DEPTH, 2, 8])
    ident_d = din("ident", [128, 128])
    amask_d = din("amask", [128, 4, 128])
    bmask_d = din("bmask", [128, 2, 128])
    smask_d = din("smask", [128, NT])
    sel_d = din("sel", [NSEG, NSEG, 128])
    w_ada = din("w_ada", [DEPTH, D, 6 * D])
    b_ada = din("b_ada", [DEPTH, 6 * D])
    w_in = din("w_in", [DEPTH, D, N_IN])
    b_in = din("b_in", [DEPTH, N_IN])
    lb_raw = din("hgrn_lb_raw", [2, DEPTH, D])
    hg_d = din("hgrn_norm_g", [DEPTH, 128])
    conv_w = din("conv_w", [DEPTH, 3, 3, D])
    conv_b = din("conv_b", [DEPTH, D])
    w_mq = din("w_mq", [DEPTH, 8, 128, 128])
    w_mk = din("w_mk", [DEPTH, 8, 128, 128])
    fbias = din("mlstm_fbias", [DEPTH, 2, 8])
    mg_d = din("mlstm_norm_g", [DEPTH, 128])
    w_ba = din("w_branch_a", [DEPTH, D, D])
    w_bb = din("w_branch_b", [DEPTH, D, D])
    w_out = din("w_out", [DEPTH, D, D])
    ln1_g = din("ln1_g", [DEPTH, D])
    ln1_b = din("ln1_b", [DEPTH, D])
    ln2_g = din("ln2_g", [DEPTH, D])
    ln2_b = din("ln2_b", [DEPTH, D])
    ffn_w1 = din("ffn_w1", [2, D, DFF])
    ffn_w3 = din("ffn_w3", [2, D, DFF])
    ffn_w2 = din("ffn_w2", [2, DFF, D])
    r_w = din("moe_router_w", [2, D, NE])
    r_b = din("moe_router_b", [2, NE])
    moe_w1 = din("moe_w1", [2, NE, D, DFF])
    moe_w3 = din("moe_w3", [2, NE, D, DFF])
    moe_w2 = din("moe_w2", [2, NE, DFF, D])

    yout = dout("yout", [NT, D])
    o_h = dout("o_h", [NSEG, DEPTH, 2, 8, 128, 128])
    o_C = dout("o_C", [NSEG, DEPTH, 2, 8, 128, 128])
    o_n = dout("o_n", [NSEG, DEPTH, 2, 8, 128])
    o_m = dout("o_m", [NSEG, DEPTH, 2, 8])

    c.init_psum()

    dbg = {}

    def dump(name, ap, shape, buf, bf=False):
        if not debug or name in dbg:
            return
        dt_ = nc.dram_tensor("dbg_" + name, list(shape), F32, kind="ExternalOutput").ap()
        dbg[name] = dt_
        bufs = buf if isinstance(buf, (list, tuple)) else [buf]
        c.dma("pool" if bf else "sp", dt_, ap, R=list(bufs), sem="dbg_" + name, is_out=True)

    xt = c.sb("xt", [128, NTILE, D])
    xb = [Buf(f"x{j}") for j in range(NTILE)]
    hT, hTb = c.tile("hT", [128, 8, NT], BF16)
    ident, identb = c.tile("ident", [128, 128])
    identh, identhb = c.tile("identh", [128, 128], BF16)
    ones32, ones32b = c.tile("ones32", [65, 128])
    onesf8, onesf8b = c.tile("onesf8", [8, 128])
    onesbf, onesbfb = c.tile("onesbf", [128, 128], BF16)
    flag, flagb = c.tile("flag", [128, 1])
    amask, amaskb = c.tile("amask", [128, 4, 128])
    bmask, bmaskb = c.tile("bmask", [128, 2, 128])
    smask, smaskb = c.tile("smask", [128, NT])
    sel, selb = c.tile("sel", [NSEG, NSEG, 128])
    actT, actTb = c.tile("actT", [128, 8, NSEG], BF16)
    lbt, lbtb = c.tile("lbt", [128, 2, DEPTH, 8])
    omlt, omltb = c.tile("omlt", [128, 2, DEPTH, 8])
    cmask = c.sb("cmask", [128, 2, 1154], BF16)
    cmaskb = Buf("cmask")

    for (t, b, src) in [(ident, identb, ident_d), (flag, flagb, flag_d), (amask, amaskb, amask_d),
                        (bmask, bmaskb, bmask_d), (sel, selb, sel_d)]:
        c.dma("sp", t[:], src, W=[b])
    c.dma("pool", identh[:], ident_d, W=[identhb])
    c.dma("sp", smask[:], smask_d, W=[smaskb])
    for i in range(2):
        c.dma("pool", cmask[:, i, :], cmask_d[i], W=[cmaskb], sem="cmask")
    c.memset("pool", ones32[:], 1.0, [ones32b])
    c.memset("pool", onesf8[:], 1.0, [onesf8b])
    c.memset("pool", onesbf[:], 1.0, [onesbfb])
    c.dma("sp", xt[:], xin.rearrange("(j p) d -> p j d", p=128), W=xb)

    with contextlib.ExitStack() as es:
        ct = es.enter_context(nc.sbuf_tensor("condt", [NSEG, D], F32)); ctb = Buf("condt")
        ca = es.enter_context(nc.sbuf_tensor("conda", [NSEG, D], F32)); cab = Buf("conda")
        c.dma("sp", ct[:], cond, W=[ctb])
        c.act(ca[:], ct[:], AF.Silu, [ctb], [cab])
        p, pb = c.ps()
        for k in range(8):
            c.tr(p[:, k * NSEG:(k + 1) * NSEG], ca[:, k * 128:(k + 1) * 128], ident[0:NSEG, 0:NSEG], [cab, identb], [pb])
        c.cp("dve", actT[:].rearrange("p k s -> p (k s)"), p[:, 0:8 * NSEG], [pb], [actTb])
        dump("ct", ct[:], [NSEG, D], ctb)
        dump("ca", ca[:], [NSEG, D], cab)
        dump("actT", actT[:], [128, 8, NSEG], actTb, bf=True)

        lr = es.enter_context(nc.sbuf_tensor("lbraw", [64, 128], F32)); lrb = Buf("lbraw")
        c.dma("sp", lr[:], lb_raw.rearrange("d l (c f) -> (d l c) f", f=128), W=[lrb])
        p, pb = c.ps()
        c.tr(p[:, 0:64], lr[:], ident[0:64, 0:64], [lrb, identb], [pb])
        ex = es.enter_context(nc.sbuf_tensor("lbex", [128, 2, DEPTH, 8], F32)); exb = Buf("lbex")
        c.act(ex[:].rearrange("p d l c -> p (d l c)"), p[:, 0:64], AF.Exp, [pb], [exb])
        sm = es.enter_context(nc.sbuf_tensor("lbsum", [128, 2, 8], F32)); smb = Buf("lbsum")
        c.tt("dve", sm[:], ex[:, :, 0, :], ex[:, :, 1, :], ALU.add, [exb], [smb])
        c.tt("dve", sm[:], sm[:], ex[:, :, 2, :], ALU.add, [exb, smb], [smb])
        c.tt("dve", sm[:], sm[:], ex[:, :, 3, :], ALU.add, [exb, smb], [smb])
        c.S.op("dve", lambda e: e.reciprocal(out=sm[:], in_=sm[:]), reads=[smb], writes=[smb])
        for l in range(DEPTH):
            c.tt("dve", ex[:, :, l, :], ex[:, :, l, :], sm[:], ALU.mult, [exb, smb], [exb])
        c.memset("dve", lbt[:, :, 0, :], 0.0, [lbtb])
        c.cp("dve", lbt[:, :, 1, :], ex[:, :, 1, :], [exb], [lbtb])
        c.tt("dve", lbt[:, :, 2, :], lbt[:, :, 1, :], ex[:, :, 2, :], ALU.add, [exb, lbtb], [lbtb])
        c.tt("dve", lbt[:, :, 3, :], lbt[:, :, 2, :], ex[:, :, 3, :], ALU.add, [exb, lbtb], [lbtb])
        c.ts("dve", omlt[:], lbt[:], -1.0, 1.0, ALU.mult, ALU.add, [lbtb], [omltb])

    S.barrier()
    r8 = Ring(c, "w8_", [128, 8, 128], 8)
    rsm = Ring(c, "wsm_", [128, 128], 4)

    def w8fn(src2d, c0, ncol=128):
        def fn(t):
            return [(t[:, :, 0:ncol], src2d[:, c0:c0 + ncol].rearrange("(k p) n -> p k n", p=128))]
        return fn

    print("SBUF after persistent:", nc.sbuf_bytes_remaining)

    for l in range(nlayers):
        c.uid = 100
        es = contextlib.ExitStack()

        def lt(name, shape, dt=F32, es=es, l=l):
            t = es.enter_context(nc.sbuf_tensor(f"L{l}_{name}", shape, dt))
            return t, Buf(f"L{l}_{name}")

        S.mark(f"L{l} params")
        bcol, bcolb = lt("bcol", [128, 80])
        brow, browb = lt("brow", [65, D])
        gbi, gbib = lt("gbi", [8, 4])
        cvw, cvwb = lt("cvw", [128, 9, 8])
        cvb, cvbb = lt("cvb", [128, 8])
        hg, hgb = lt("hg", [128, 1])
        mgbc, mgbcb = lt("mgbc", [128, 128])
        adab, adabb = lt("adab", [128, 48])
        modT, modTb = lt("modT", [128, 48, NSEG])
        gtok, gtokb = lt("gtok", [NSEG, 2, D])
        fbt, fbtb = lt("fbt", [8, 2])

        with contextlib.ExitStack() as e2:
            raw, rawb = (e2.enter_context(nc.sbuf_tensor(f"L{l}_raw", [80, 128], F32)), Buf("raw"))
            c.dma("sp", raw[0:64, :], b_in[l, 0:8192].rearrange("(c f) -> c f", f=128), W=[rawb], sem="raw")
            c.dma("sp", raw[64:80, :], b_in[l, 8224:N_IN].rearrange("(c f) -> c f", f=128), W=[rawb], sem="raw")
            p, pb = c.ps()
            c.tr(p[:, 0:80], raw[:], ident[0:80, 0:80], [rawb, identb], [pb])
            c.cp("dve", bcol[:], p[:, 0:80], [pb], [bcolb])
            raw2, raw2b = (e2.enter_context(nc.sbuf_tensor(f"L{l}_raw2", [80, 128], F32)), Buf("raw2"))
            c.dma("sp", raw2[0:72, :], conv_w[l].rearrange("a b (c f) -> (a b c) f", f=128), W=[raw2b], sem="raw2")
            c.dma("sp", raw2[72:80, :], conv_b[l].rearrange("(c f) -> c f", f=128), W=[raw2b], sem="raw2")
            p, pb = c.ps()
            c.tr(p[:, 0:80], raw2[:], ident[0:80, 0:80], [raw2b, identb], [pb])
            c.cp("dve", cvw[:].rearrange("p t c -> p (t c)"), p[:, 0:72], [pb], [cvwb])
            c.cp("dve", cvb[:], p[:, 72:80], [pb], [cvbb])
            raw3, raw3b = (e2.enter_context(nc.sbuf_tensor(f"L{l}_raw3", [48, 128], F32)), Buf("raw3"))
            c.dma("sp", raw3[:], b_ada[l].rearrange("(c f) -> c f", f=128), W=[raw3b])
            p, pb = c.ps()
            c.tr(p[:, 0:48], raw3[:], ident[0:48, 0:48], [raw3b, identb], [pb])
            c.cp("dve", adab[:], p[:, 0:48], [pb], [adabb])
        S.barrier()
        for gi, off in enumerate([3072, 6144, 7168]):
            c.dma("sp", brow[32 * gi:32 * gi + 1, :], b_in[l:l + 1, off:off + D], W=[browb], sem="brow")
        with nc.allow_non_contiguous_dma(reason="tiny param loads"):
            c.dma("sp", gbi[:], b_in[l, 8192:8224].rearrange("(g h) -> h g", h=8), W=[gbib])
            c.dma("sp", hg[:], hg_d[l].rearrange("(f o) -> f o", o=1), W=[hgb])
            c.dma("sp", fbt[:], fbias[l].rearrange("d h -> h d"), W=[fbtb])
        c.dma("sp", mgbc[:], mg_d[l:l + 1, :].partition_broadcast(128), W=[mgbcb])
        gb_es = contextlib.ExitStack()
        gbrow = gb_es.enter_context(nc.sbuf_tensor(f"L{l}_gbrow", [NSEG, 2, D], F32)); gbrowb = Buf("gbrow")
        for i, off in enumerate([2048, 5120]):
            c.dma("sp", gbrow[:, i, :], b_ada[l:l + 1, off:off + D].partition_broadcast(NSEG), W=[gbrowb], sem="gbrow")

        S.mark(f"L{l} adaln")
        pf = Prefetch(r8, [w8fn(w_ada[l], n * 128) for n in range(48)])
        gp = c.ps_reserve(4)
        for n in range(48):
            wt, wb = pf.next()
            kind = n // 8
            if kind in (2, 5):
                gi = 0 if kind == 2 else 1
                col = (n % 8) * 128
                pp, ppb = gp[gi * 2 + col // 512]
                for k in range(8):
                    c.mm(pp[0:NSEG, col % 512:col % 512 + 128], actT[:, k, :], wt[:, k, :], k == 0, k == 7, [actTb, wb], [ppb])
            else:
                p, pb = c.ps()
                for k in range(8):
                    c.mm(p[:, 0:NSEG], wt[:, k, :], actT[:, k, :], k == 0, k == 7, [actTb, wb], [pb])
                c.act(modT[:, n, :], p[:, 0:NSEG], AF.Identity, [pb, adabb], [modTb], bias=adab[:, n:n + 1])
        for gi in range(2):
            for hf in range(2):
                pp, ppb = gp[gi * 2 + hf]
                c.tt("dve", gtok[:, gi, hf * 512:(hf + 1) * 512], pp[0:NSEG, :], gbrow[:, gi, hf * 512:(hf + 1) * 512], ALU.add, [ppb, gbrowb], [gtokb])
        c.ps_release()
        gb_es.close()
        S.barrier()
        sc1, sc1b = lt("sc1", [128, 2, 8, NSEG])
        c.ts("dve", sc1[:, 0], modT[:, 8:16, :], 1.0, None, ALU.add, None, [modTb], [sc1b])
        c.ts("dve", sc1[:, 1], modT[:, 32:40, :], 1.0, None, ALU.add, None, [modTb], [sc1b])

        def transpose_mod(which):
            shoff = 0 if which == 0 else 24
            for j in range(NTILE):
                seg = j // 2
                for k4 in range(2):
                    p, pb = c.ps()
                    for kk in range(4):
                        k = k4 * 4 + kk
                        c.tr(p[:, kk * 128:(kk + 1) * 128], xt[:, j, k * 128:(k + 1) * 128], ident[:], [xb[j], identb], [pb])
                    for kk in range(4):
                        k = k4 * 4 + kk
                        c.act(hT[:, k, j * 128:(j + 1) * 128], p[:, kk * 128:(kk + 1) * 128], AF.Identity, [pb, modTb, sc1b], [hTb],
                              bias=modT[:, shoff + k, seg:seg + 1], scale=sc1[:, which, k, seg:seg + 1])

        def gbc_build(gi, seg, dst, dstb):
            for hf in range(2):
                p, pb = c.ps()
                c.mm(p[:, :], sel[:, seg, :], gtok[:, gi, hf * 512:(hf + 1) * 512], True, True, [selb, gtokb], [pb])
                c.cp("act", dst[:, hf * 512:(hf + 1) * 512], p[:, :], [pb], [dstb])

        def load_lnv(lnv, lnvb, which):
            for i, v in enumerate([ln1_g, ln1_b] if which == 0 else [ln2_g, ln2_b]):
                c.dma("sp", lnv[:, i, :], v[l:l + 1, :].partition_broadcast(128), W=[lnvb], sem="lnv")

        def layer_norm_tile(j, vt, vb, lnv, lnvb, stats, statsb, mv, mvb):
            for hf in range(2):
                S.op("dve", lambda e, hf=hf: e.bn_stats(out=stats[:, hf, :], in_=vt[:, hf * 512:(hf + 1) * 512]), reads=[vb], writes=[statsb])
            S.op("dve", lambda e: e.bn_aggr(out=mv[:, 0:2], in_=stats[:]), reads=[statsb], writes=[mvb])
            c.act(mv[:, 2:3], mv[:, 1:2], AF.Sqrt, [mvb, epsb], [mvb], bias=eps5[:, 0:1])
            S.op("dve", lambda e: e.reciprocal(out=mv[:, 3:4], in_=mv[:, 2:3]), reads=[mvb], writes=[mvb])
            c.ts("dve", vt[:], vt[:], mv[:, 0:1], mv[:, 3:4], ALU.subtract, ALU.mult, [vb, mvb], [vb])
            c.tt("pool", vt[:], vt[:], lnv[:, 0, :], ALU.mult, [vb, lnvb], [vb])
            c.tt("pool", xt[:, j, :], vt[:], lnv[:, 1, :], ALU.add, [vb, lnvb], [xb[j]])

        eps5, epsb = lt("eps5", [128, 2])
        c.memset("pool", eps5[:, 0:1], 1e-5, [epsb])
        c.memset("pool", eps5[:, 1:2], 1e-6, [epsb])

        dump("modT", modT[:], [128, 48, NSEG], modTb)
        dump("gtok", gtok[:], [NSEG, 2, D], gtokb)
        dump("bcol", bcol[:], [128, 80], bcolb)
        S.mark(f"L{l} transpose0")
        transpose_mod(0)
        dump("hT", hT[:], [128, 8, NT], hTb, bf=True)

        mx = contextlib.ExitStack()

        def mt(name, shape, dt=F32, mx=mx, l=l):
            t = mx.enter_context(nc.sbuf_tensor(f"L{l}m_{name}", shape, dt))
            return t, Buf(f"L{l}m_{name}")

        yaT = mx.enter_context(nc.sbuf_tensor(f"L{l}_yaT", [128, 8, NT], BF16))
        yaTb = [Buf(f"ya{h}") for h in range(8)]

        wl = w_in[l]

        def proj_fm(wt, wb, evac):
            for pi, (t0, n) in enumerate(PIECES):
                p, pb = c.ps()
                for k in range(8):
                    c.mm(p[:, 0:n], wt[:, k, :], hT[:, k, t0:t0 + n], k == 0, k == 7, [wb, hTb], [pb])
                evac(pi, t0, n, p, pb)

        def proj_tm(wt, wb, gi, h, evac):
            for j0 in range(0, NTILE, 4):
                p, pb = c.ps()
                js = list(range(j0, min(j0 + 4, NTILE)))
                for ji, j in enumerate(js):
                    o = p[:, ji * 128:(ji + 1) * 128]
                    for k in range(8):
                        c.mm(o, hT[:, k, j * 128:(j + 1) * 128], wt[:, k, :], k == 0, False, [wb, hTb], [pb])
                    c.mm(o, ones32[32 * gi:32 * gi + 1, 0:128], brow[32 * gi:32 * gi + 1, h * 128:(h + 1) * 128], False, True, [ones32b, browb], [pb])
                evac(j0, len(js), p, pb)

        hg_es = contextlib.ExitStack()

        def ht(name, shape, dt=F32, hg_es=hg_es, l=l):
            t = hg_es.enter_context(nc.sbuf_tensor(f"L{l}h_{name}", shape, dt))
            return t, Buf(f"L{l}h_{name}")

        q32, q32b = ht("q32", [128, NT])
        gate, gateb = ht("gate", [128, NT], BF16)
        vtok, vtokb = ht("vtok", [128, NTILE, 128], BF16)
        sg, sgb = ht("sg", [128, NT])
        lg, lgb = sg, sgb
        bb, bbb = sg, sgb
        kk_, kkb = ht("kk", [128, NT], BF16)
        tmpa, tmpab = ht("tmpa", [128, NT])
        tmpe, tmpeb = ht("tmpe", [128, NT])
        qtil = [ht(f"qtil{d}", [128, NT], BF16) for d in range(2)]
        ktil = [ht(f"ktil{d}", [128, NT], BF16) for d in range(2)]
        q64 = [ht(f"q64{d}", [128, NT], BF16) for d in range(2)]
        khT, khTb = kk_, kkb
        khat = [ht(f"khat{d}", [128, 2, NTILE, 128], BF16) for d in range(2)]
        for d in range(2):
            c.memset("pool", khat[d][0][:], 0.0, [khat[d][1]])
        ecol = [ht(f"ecol{d}", [128, 2, 20]) for d in range(2)]
        Sst = [ht(f"Sst{d}", [128, 128]) for d in range(2)]
        Sst2 = [ht(f"Sst2{d}", [128, 128]) for d in range(2)]
        Sbf = [ht(f"Sbf{d}", [128, 20, 128], BF16) for d in range(2)]
        stage = [ht(f"stage{i}", [128, 128]) for i in range(2)]
        attsb2 = [ht(f"attsb{i}", [128, 4, 128], BF16) for i in range(2)]
        sq, sqb = ht("sq", [128, 512], BF16)
        rstd, rstdb = ht("rstd", [128, 512])
        t1, t1b = ht("t1", [128, 512])
        stg_i = [0]

        print("SBUF after hgrn alloc:", nc.sbuf_bytes_remaining)
        hcols = [0, 3072, 4096, 1024, 2048]
        fns = []
        for h in range(8):
            for g in range(5):
                fns.append(w8fn(wl, hcols[g] + h * 128))
        pf = Prefetch(r8, fns)

        for h in range(8):
            S.mark(f"L{l} hgrn h{h} proj+gates")
            wt, wb = pf.next()

            def ev_q(pi, t0, n, p, pb, h=h):
                c.act(q32[:, t0:t0 + n], p[:, 0:n], AF.Identity, [pb, bcolb], [q32b], bias=bcol[:, h:h + 1])
            proj_fm(wt, wb, ev_q)
            wt, wb = pf.next()

            def ev_v(j0, nj, p, pb):
                c.cp("act", vtok[:, j0:j0 + nj, :].rearrange("p j f -> p (j f)"), p[:, 0:nj * 128], [pb], [vtokb])
            proj_tm(wt, wb, 0, h, ev_v)
            wt, wb = pf.next()

            def ev_g(pi, t0, n, p, pb, h=h):
                c.act(gate[:, t0:t0 + n], p[:, 0:n], AF.Silu, [pb, bcolb], [gateb], bias=bcol[:, 32 + h:33 + h])
            proj_fm(wt, wb, ev_g)

            for d in range(2):
                wt, wb = pf.next()

                def ev_z(pi, t0, n, p, pb, h=h, d=d):
                    c.act(sg[:, t0:t0 + n], p[:, 0:n], AF.Sigmoid, [pb, bcolb], [sgb], bias=bcol[:, 8 + 8 * d + h:9 + 8 * d + h])
                proj_fm(wt, wb, ev_z)
                c.ts("dve", sg[:], sg[:], omlt[:, d, l, h:h + 1], lbt[:, d, l, h:h + 1], ALU.mult, ALU.add, [sgb, omltb, lbtb], [sgb])
                c.ts("pool", kk_[:], sg[:], -1.0, 1.0, ALU.mult, ALU.add, [sgb], [kkb])
                c.act(sg[:], sg[:], AF.Ln, [sgb], [sgb])
                if d == 0:
                    S.op("dve", lambda e: e.tensor_tensor_scan(out=sg[:], data0=smask[:], data1=sg[:], initial=0.0, op0=ALU.mult, op1=ALU.add),
                         reads=[smaskb, sgb], writes=[sgb])
                else:
                    S.op("dve", lambda e: e.tensor_tensor_scan(out=sg[:, ::-1], data0=smask[:], data1=sg[:, ::-1], initial=0.0, op0=ALU.mult, op1=ALU.add),
                         reads=[smaskb, sgb], writes=[sgb])
                if h == 0:
                    dump(f"b{d}", sg[:], [128, NT], sgb)
                bv = bb[:].rearrange("p (c j) -> p c j", j=64)
                last = 63 if d == 0 else 0
                ec, ecb = ecol[d]
                S.op("act", lambda e, ec=ec, bv=bv, last=last: e.activation(out=ec[:, 1, :], in_=bv[:, :, last], func=AF.Exp), reads=[bbb], writes=[ecb])
                c.act(tmpe[:], sg[:], AF.Exp, [sgb], [tmpeb])
                c.tt("pool", q64[d][0][:], q32[:], tmpe[:], ALU.mult, [q32b, tmpeb], [q64[d][1]])
                b4 = bb[:].rearrange("p (c h j) -> p c h j", h=2, j=32)
                t4 = tmpa[:].rearrange("p (c h j) -> p c h j", h=2, j=32)
                if d == 0:
                    c.cp("pool", t4[:, :, 0, :], b4[:, :, 0, :], [bbb], [tmpab])
                    c.tt("dve", t4[:, :, 1, :], b4[:, :, 1, :], b4[:, :, 0, 31:32].to_broadcast([128, 20, 32]), ALU.subtract, [bbb], [tmpab])
                else:
                    c.cp("pool", t4[:, :, 1, :], b4[:, :, 1, :], [bbb], [tmpab])
                    c.tt("dve", t4[:, :, 0, :], b4[:, :, 0, :], b4[:, :, 1, 0:1].to_broadcast([128, 20, 32]), ALU.subtract, [bbb], [tmpab])
                c.act(tmpe[:], tmpa[:], AF.Exp, [tmpab], [tmpeb])
                qt, qtb = qtil[d]
                c.tt("pool", qt[:], q32[:], tmpe[:], ALU.mult, [q32b, tmpeb], [qtb])
                c.act(tmpe[:], tmpa[:], AF.Exp, [tmpab], [tmpeb], scale=-1.0)
                kt, ktb = ktil[d]
                c.tt("pool", kt[:], kk_[:], tmpe[:], ALU.mult, [kkb, tmpeb], [ktb])
                tv = tmpa[:].rearrange("p (c j) -> p c j", j=64)
                c.tt("dve", tv, bv, bv[:, :, last:last + 1].to_broadcast([128, 20, 64]), ALU.subtract, [bbb], [tmpab])
                c.act(tmpe[:], tmpa[:], AF.Exp, [tmpab], [tmpeb], scale=-1.0)
                c.tt("pool", khT[:], kk_[:], tmpe[:], ALU.mult, [kkb, tmpeb], [khTb])
                kh, khb = khat[d]
                for j0 in (0, 4, 8):
                    p, pb = c.ps()
                    pv = p[:].bitcast(BF16)
                    nj = min(4, NTILE - j0)
                    for ji in range(nj):
                        j = j0 + ji
                        c.tr(pv[:, ji * 128:(ji + 1) * 128], khT[:, j * 128:(j + 1) * 128], identh[:], [khTb, identhb], [pb])
                    c.cp("act", kh[0:64, 0, j0:j0 + nj, :].rearrange("p j f -> p (j f)"), pv[0:64, 0:nj * 128], [pb], [khb])
                    c.cp("act", kh[64:128, 1, j0:j0 + nj, :].rearrange("p j f -> p (j f)"), pv[64:128, 0:nj * 128], [pb], [khb])

            if h == 0:
                dump("q32", q32[:], [128, NT], q32b)
                dump("gate", gate[:], [128, NT], gateb, bf=True)
                dump("vtok", vtok[:], [128, NTILE, 128], vtokb, bf=True)
                for d in range(2):
                    dump(f"qtil{d}", qtil[d][0][:], [128, NT], qtil[d][1], bf=True)
                    dump(f"ktil{d}", ktil[d][0][:], [128, NT], ktil[d][1], bf=True)
                    dump(f"khat{d}", khat[d][0][:], [128, 2, NTILE, 128], khat[d][1], bf=True)
                    dump(f"q64{d}", q64[d][0][:], [128, NT], q64[d][1], bf=True)
            S.mark(f"L{l} hgrn h{h} chain")
            for d in range(2):
                sts = [Sst[d], Sst2[d]]
                cur = 0
                sbf, sbfb = Sbf[d]
                ec, ecb = ecol[d]
                kh, khb = khat[d]
                order = list(range(20)) if d == 0 else list(range(19, -1, -1))
                for g in range(5):
                    chunks = order[g * 4:(g + 1) * 4]
                    p, pb = c.ps()
                    for i, cc in enumerate(chunks):
                        j = cc // 2
                        r0 = (cc % 2) * 64
                        c.mm(p[:, i * 128:(i + 1) * 128], kh[:, cc % 2, j, :], vtok[:, j, :], True, True, [khb, vtokb], [pb])
                    for i, cc in enumerate(chunks):
                        seg = cc // 4
                        entry = (i == 0)
                        final = (i == 3)
                        st_, stb = sts[cur]
                        if entry:
                            if seg == 4:
                                c.memset("dve", st_[:], 0.0, [stb])
                            elif (d == 0 and seg == 0) or (d == 1 and seg == 3):
                                c.dma("sp", st_[:], hinit[l, d, h], W=[stb], sem=f"hin{d}")
                            else:
                                c.ts("dve", st_[:], st_[:], flag[:, 0:1], None, ALU.mult, None, [stb, flagb], [stb])
                        c.cp("act", sbf[:, cc, :], st_[:], [stb], [sbfb])
                        nx, nxb = sts[1 - cur]
                        c.stt("dve", nx[:], st_[:], ec[:, 1, cc:cc + 1], p[:, i * 128:(i + 1) * 128], ALU.mult, ALU.add, [stb, ecb, pb], [nxb])
                        cur = 1 - cur
                        if final:
                            sgt, sgtb = stage[stg_i[0] % 2]
                            c.cp("pool", sgt[:], nx[:], [nxb], [sgtb])
                            c.dma("sp", o_h[seg, l, d, h], sgt[:], R=[sgtb], sem=f"ostage{stg_i[0] % 2}", is_out=True)
                            stg_i[0] += 1

            if h == 0:
                for d in range(2):
                    dump(f"Sbf{d}", Sbf[d][0][:], [128, 20, 128], Sbf[d][1], bf=True)
            S.mark(f"L{l} hgrn h{h} out")
            def emit_att(j):
                pa, pab = c.ps()
                tsl = slice(j * 128, (j + 1) * 128)
                for d in range(2):
                    c.mm(pa[:, d * 128:(d + 1) * 128], ktil[d][0][:, tsl], qtil[d][0][:, tsl], True, True, [ktil[d][1], qtil[d][1]], [pab])
                    c.mm(pa[:, 256 + d * 128:256 + (d + 1) * 128], ktil[d][0][:, tsl], q64[d][0][:, tsl], True, True, [ktil[d][1], q64[d][1]], [pab])
                a_, a_b = attsb2[j % 2]
                c.tt("dve", a_[:].rearrange("p d t -> p (d t)"), pa[:, 0:512], amask[:].rearrange("p d t -> p (d t)"), ALU.mult, [pab, amaskb], [a_b])

            emit_att(0)
            for pi, (t0, n) in enumerate(PIECES):
                po, pob = c.ps()
                for ji in range(n // 128):
                    j = t0 // 128 + ji
                    if j + 1 < NTILE:
                        emit_att(j + 1)
                    a_, a_b = attsb2[j % 2]
                    o = po[:, ji * 128:(ji + 1) * 128]
                    for i4 in range(4):
                        c.mm(o, vtok[:, j, :], a_[:, i4, :], i4 == 0, False, [vtokb, a_b], [pob])
                    for d in range(2):
                        for cc in (2 * j, 2 * j + 1):
                            oc = po[:, ji * 128 + (cc % 2) * 64: ji * 128 + (cc % 2) * 64 + 64]
                            c.mm(oc, Sbf[d][0][:, cc, :], q64[d][0][:, cc * 64:(cc + 1) * 64], False, (d == 1 and cc == 2 * j + 1),
                                 [Sbf[d][1], q64[d][1]], [pob])
                c.act(sq[:, 0:n], po[:, 0:n], AF.Square, [pob], [sqb])
                pss, pssb = c.ps()
                c.mm(pss[:, 0:n], onesbf[:], sq[:, 0:n], True, True, [onesbfb, sqb], [pssb])
                c.act(rstd[:, 0:n], pss[:, 0:n], AF.Sqrt, [pssb, epsb], [rstdb], bias=eps5[:, 1:2], scale=1.0 / 128)
                S.op("dve", lambda e, n=n: e.reciprocal(out=rstd[:, 0:n], in_=rstd[:, 0:n]), reads=[rstdb], writes=[rstdb])
                c.stt("dve", t1[:, 0:n], po[:, 0:n], hg[:, 0:1], rstd[:, 0:n], ALU.mult, ALU.mult, [pob, hgb, rstdb], [t1b])
                c.tt("pool", yaT[:, h, t0:t0 + n], t1[:, 0:n], gate[:, t0:t0 + n], ALU.mult, [t1b, gateb], [yaTb[h]])
        dump("yaT", yaT[:], [128, 8, NT], yaTb, bf=True)
        hg_es.close()
        S.barrier()
        ybT = mx.enter_context(nc.sbuf_tensor(f"L{l}_ybT", [128, 8, NT], BF16))
        ybTb = [Buf(f"yb{h}") for h in range(8)]

        ml_es = contextlib.ExitStack()

        def mlt(name, shape, dt=F32, ml_es=ml_es, l=l):
            t = ml_es.enter_context(nc.sbuf_tensor(f"L{l}b_{name}", shape, dt))
            return t, Buf(f"L{l}b_{name}")

        S.mark(f"L{l} mlstm gates")
        gw, gwb = mlt("gw", [128, 8, 32], BF16)
        c.dma("pool", gw[:], wl[:, 8192:8224].rearrange("(k p) n -> p k n", p=128), W=[gwb], sem="gw")
        utok, utokb = mlt("utok", [128, NTILE, 32])
        abc, abcb = mlt("abc", [128, 2, 8, NTILE])
        mfin, mfinb = mlt("mfin", [8, 2, NSEG])
        ncols, ncolsb = mlt("ncols", [128, NSEG, 2, 8])
        c.memset("pool", ncols[:], 0.0, [ncolsb])
        for d in range(2):
            with contextlib.ExitStack() as ge:
                def gt(name, shape, dt=F32, ge=ge):
                    t = ge.enter_context(nc.sbuf_tensor(f"L{l}g_{name}", shape, dt))
                    return t, Buf(f"L{l}g_{name}")
                ig, igb = gt(f"ig{d}", [8, NT])
                lp, lpb = gt(f"lp{d}", [8, NT])
                P_, Pb = gt(f"P{d}", [8, NT])
                G_, Gb = ig, igb
                nb, nbb = gt(f"nb{d}", [8, 1])
                gmax, gmaxb = gt(f"gmax{d}", [8, NTILE])
                Mt, Mtb = gt(f"M{d}", [8, NTILE])
                ms, msb = gt(f"ms{d}", [8, NTILE])
                A_, Ab = gt(f"A{d}", [8, NTILE])
                mcur, mcurb = gt(f"mcur{d}", [8, 1])
                arhs, arhsb = gt(f"arhs{d}", [8, 8, NTILE])
                c.tt("dve", nb[:], gbi[:, 2 * d + 1:2 * d + 2], fbt[:, d:d + 1], ALU.add, [gbib, fbtb], [nbb])
                c.ts("dve", nb[:], nb[:], -1.0, None, ALU.mult, None, [nbb], [nbb])
                for pi, (t0, n) in enumerate(PIECES):
                    p, pb = c.ps()
                    for k in range(8):
                        c.mm(p[0:8, 0:n], gw[:, k, 16 * d:16 * d + 8], hT[:, k, t0:t0 + n], k == 0, k == 7, [gwb, hTb], [pb])
                    c.act(ig[:, t0:t0 + n], p[0:8, 0:n], AF.Identity, [pb, gbib], [igb], bias=gbi[:, 2 * d:2 * d + 1])
                    p, pb = c.ps()
                    for k in range(8):
                        c.mm(p[0:8, 0:n], gw[:, k, 16 * d + 8:16 * d + 16], hT[:, k, t0:t0 + n], k == 0, k == 7, [gwb, hTb], [pb])
                    c.act(lp[:, t0:t0 + n], p[0:8, 0:n], AF.Exp, [pb, nbb], [lpb], bias=nb[:, 0:1], scale=-1.0)
                c.act(lp[:], lp[:], AF.Ln, [lpb], [lpb], bias=1.0)
                for cc in range(NTILE):
                    if d == 0:
                        S.op("dve", lambda e, P_=P_, lp=lp, cc=cc: e.tensor_tensor_scan(out=P_[:, cc * 128:(cc + 1) * 128], data0=onesf8[:, :], data1=lp[:, cc * 128:(cc + 1) * 128], initial=0.0, op0=ALU.mult, op1=ALU.add),
                             reads=[onesf8b, lpb], writes=[Pb])
                    else:
                        S.op("dve", lambda e, P_=P_, lp=lp, cc=cc: e.tensor_tensor_scan(out=P_[:, cc * 128:(cc + 1) * 128][:, ::-1], data0=onesf8[:, :], data1=lp[:, cc * 128:(cc + 1) * 128][:, ::-1], initial=0.0, op0=ALU.mult, op1=ALU.add),
                             reads=[onesf8b, lpb], writes=[Pb])
                c.tt("dve", G_[:], ig[:], P_[:], ALU.add, [igb, Pb], [Gb])
                S.op("dve", lambda e, gmax=gmax, G_=G_: e.tensor_reduce(out=gmax[:], in_=G_[:].rearrange("p (c j) -> p c j", j=128), axis=AX.X, op=ALU.max),
                     reads=[Gb], writes=[gmaxb])
                Pv = P_[:].rearrange("p (c j) -> p c j", j=128)
                lastj = 127 if d == 0 else 0
                order = list(range(NTILE)) if d == 0 else list(range(NTILE - 1, -1, -1))
                for cc in order:
                    seg = cc // 2
                    entry = (cc % 2 == 0) if d == 0 else (cc % 2 == 1)
                    final = (cc % 2 == 1) if d == 0 else (cc % 2 == 0)
                    if entry:
                        if seg == 4:
                            c.memset("dve", mcur[:], 0.0, [mcurb])
                        elif (d == 0 and seg == 0) or (d == 1 and seg == 3):
                            with nc.allow_non_contiguous_dma(reason="tiny"):
                                c.dma("sp", mcur[:], minit[l, d].rearrange("(h o) -> h o", o=1), W=[mcurb], sem=f"min{d}")
                        else:
                            c.ts("dve", mcur[:], mcur[:], flag[0:8, 0:1], None, ALU.mult, None, [mcurb, flagb], [mcurb])
                    c.cp("dve", ms[:, cc:cc + 1], mcur[:], [mcurb], [msb])
                    c.tt("dve", Mt[:, cc:cc + 1], mcur[:], gmax[:, cc:cc + 1], ALU.max, [mcurb, gmaxb], [Mtb])
                    c.tt("dve", mcur[:], Mt[:, cc:cc + 1], Pv[:, cc, lastj:lastj + 1], ALU.subtract, [Mtb, Pb], [mcurb])
                    if final:
                        c.cp("dve", mfin[:, d, seg:seg + 1], mcur[:], [mcurb], [mfinb])
                c.tt("dve", A_[:], ms[:], Mt[:], ALU.subtract, [msb, Mtb], [Ab])
                c.act(A_[:], A_[:], AF.Exp, [Ab], [Ab])
                Gv = G_[:].rearrange("p (c j) -> p c j", j=128)
                Mb_ = Mt[:].rearrange("p (c o) -> p c o", o=1).to_broadcast([8, NTILE, 128])
                c.tt("dve", Gv, Gv, Mb_, ALU.subtract, [Gb, Mtb], [Gb])
                c.tt("dve", Pv, Pv, Mb_, ALU.subtract, [Pb, Mtb], [Pb])
                c.act(G_[:], G_[:], AF.Exp, [Gb], [Gb])
                c.act(P_[:], P_[:], AF.Exp, [Pb], [Pb])
                for j in range(NTILE):
                    p, pb = c.ps()
                    c.tr(p[:, 0:8], G_[:, j * 128:(j + 1) * 128], ident[0:8, 0:8], [Gb, identb], [pb])
                    c.tr(p[:, 8:16], P_[:, j * 128:(j + 1) * 128], ident[0:8, 0:8], [Pb, identb], [pb])
                    c.cp("dve", utok[:, j, d * 16:(d + 1) * 16], p[:, 0:16], [pb], [utokb])
                c.tt("dve", arhs[:], A_[:].rearrange("p (o c) -> p o c", o=1).to_broadcast([8, 8, NTILE]),
                     ident[0:8, 0:8].rearrange("p (h o) -> p h o", o=1).to_broadcast([8, 8, NTILE]), ALU.mult, [Ab, identb], [arhsb])
                p, pb = c.ps()
                c.mm(p[:, 0:80], onesf8[:, :], arhs[:].rearrange("p h c -> p (h c)"), True, True, [arhsb, onesf8b], [pb])
                c.cp("dve", abc[:, d].rearrange("p h c -> p (h c)"), p[:, 0:80], [pb], [abcb])
            S.barrier()
        with nc.allow_non_contiguous_dma(reason="tiny m out"):
            for seg in range(NSEG):
                for d in range(2):
                    c.dma("sp", o_m[seg, l, d].rearrange("(h o) -> h o", o=1), mfin[:, d, seg:seg + 1], R=[mfinb], sem="om", is_out=True)
        for d in range(2):
            c.ts("dve", utok[:, :, d * 16:d * 16 + 8], utok[:, :, d * 16:d * 16 + 8], 128.0 ** -0.5, None, ALU.mult, None, [utokb], [utokb])

        dump("utok", utok[:], [128, NTILE, 32], utokb)
        dump("abc", abc[:], [128, 2, 8, NTILE], abcb)
        dump("mfin", mfin[:], [8, 2, NSEG], mfinb)
        xcA, xcAb = mlt("xcA", [128, 3, 1154], BF16)
        xcB, xcBb = mlt("xcB", [128, 1, 258], BF16)
        c.memset("pool", xcA[:], 0.0, [xcAb])
        c.memset("pool", xcB[:], 0.0, [xcBb])
        dgw, dgwb = mlt("dgw", [128, 9, 128], BF16)
        dgf, dgfb = mlt("dgf", [128, 9, 128], BF16)
        wcol, wcolb = mlt("wcol", [128, 9])
        caT, caTb = mlt("caT", [128, NT], BF16)
        qbT, qbTb = mlt("qbT", [128, NT], BF16)
        kbT, kbTb = mlt("kbT", [128, NT], BF16)
        kp, kpb = mlt("kp", [128, 2, NTILE, 128], BF16)
        vaug, vaugb = mlt("vaug", [128, NTILE, 129], BF16)
        c.memset("pool", vaug[:], 1.0, [vaugb])
        og, ogb = mlt("og", [128, NTILE, 128], BF16)
        Chat = [mlt(f"Chat{d}", [128, 129]) for d in range(2)]
        Chat2 = [mlt(f"Chat2{d}", [128, 129]) for d in range(2)]
        Cbf = [mlt(f"Cbf{d}", [128, NTILE, 129], BF16) for d in range(2)]
        ssb, ssbb = mlt("ssb", [128, 128], BF16)
        hacc, haccb = mlt("hacc", [128, 128])
        dn, dnb = mlt("dn", [128, 4])
        bst, bstb = mlt("bst", [128, 6])
        bmv, bmvb = mlt("bmv", [128, 4])
        ybk, ybkb = mlt("ybk", [128, 128], BF16)
        y2, y2b = mlt("y2", [128, 128])
        cstage = [mlt(f"cstage{i}", [128, 128]) for i in range(2)]
        cst_i = [0]

        fns = []
        for h in range(8):
            for g in range(3):
                fns.append(w8fn(wl, 5120 + g * 1024 + h * 128))
        pf = Prefetch(r8, fns)

        for h in range(8):
            S.mark(f"L{l} mlstm h{h} proj+conv")
            wt, wb = pf.next()

            def ev_qk(pi, t0, n, p, pb, h=h):
                if pi < 2:
                    c.act(xcA[:, 0, PADA + t0:PADA + t0 + n], p[:, 0:n], AF.Identity, [pb, bcolb], [xcAb], bias=bcol[:, 40 + h:41 + h])
                else:
                    c.act(xcB[:, 0, 1:257], p[:, 0:n], AF.Identity, [pb, bcolb], [xcBb], bias=bcol[:, 40 + h:41 + h])
            proj_fm(wt, wb, ev_qk)
            for m_ in range(2):
                c.tt("pool", xcA[:, 1 + m_, :], xcA[:, 0, :], cmask[:, m_, :], ALU.mult, [xcAb, cmaskb], [xcAb])
            c.cp("dve", wcol[:], cvw[:, :, h], [cvwb], [wcolb])
            for tap in range(9):
                c.ts("dve", dgw[:, tap, :], identh[:], wcol[:, tap:tap + 1], None, ALU.mult, None, [identhb, wcolb], [dgwb])
            c.ts("dve", wcol[:, 0:3], wcol[:, 0:3], flag[:, 0:1], None, ALU.mult, None, [wcolb, flagb], [wcolb])
            c.ts("dve", wcol[:, 6:9], wcol[:, 6:9], flag[:, 0:1], None, ALU.mult, None, [wcolb, flagb], [wcolb])
            for tap in (0, 1, 2, 6, 7, 8):
                c.ts("dve", dgf[:, tap, :], identh[:], wcol[:, tap:tap + 1], None, ALU.mult, None, [identhb, wcolb], [dgfb])
            for pi, (t0, n) in enumerate(PIECES):
                p, pb = c.ps()
                if pi < 2:
                    taps = [(kh_, kw_) for kh_ in range(3) for kw_ in range(3)]
                    for ti, (kh_, kw_) in enumerate(taps):
                        off = (kh_ - 1) * 64 + (kw_ - 1)
                        srcidx = {0: 1, 1: 0, 2: 2}[kw_]
                        wsrc = dgw if kh_ == 1 else dgf
                        wsrcb = dgwb if kh_ == 1 else dgfb
                        c.mm(p[:, 0:n], wsrc[:, kh_ * 3 + kw_, :], xcA[:, srcidx, PADA + t0 + off:PADA + t0 + off + n], ti == 0, ti == 8, [wsrcb, xcAb], [pb])
                else:
                    for kw_ in range(3):
                        c.mm(p[:, 0:n], dgw[:, 3 + kw_, :], xcB[:, 0, kw_:kw_ + 256], kw_ == 0, kw_ == 2, [dgwb, xcBb], [pb])
                c.act(caT[:, t0:t0 + n], p[:, 0:n], AF.Silu, [pb, cvbb], [caTb], bias=cvb[:, h:h + 1])
            wq, wqb = rsm.load(lambda t, h=h: [(t[:], w_mq[l, h])])
            wk, wkb = rsm.load(lambda t, h=h: [(t[:], w_mk[l, h])])
            for pi, (t0, n) in enumerate(PIECES):
                p, pb = c.ps()
                c.mm(p[:, 0:n], wq[:], caT[:, t0:t0 + n], True, True, [wqb, caTb], [pb])
                c.cp("act", qbT[:, t0:t0 + n], p[:, 0:n], [pb], [qbTb])
                p, pb = c.ps()
                c.mm(p[:, 0:n], wk[:], caT[:, t0:t0 + n], True, True, [wkb, caTb], [pb])
                c.cp("dve", kbT[:, t0:t0 + n], p[:, 0:n], [pb], [kbTb])
            for j0 in (0, 4, 8):
                p, pb = c.ps()
                nj = min(4, NTILE - j0)
                for ji in range(nj):
                    j = j0 + ji
                    c.mm(p[:, ji * 128:(ji + 1) * 128], caT[:, j * 128:(j + 1) * 128], wk[:], True, True, [caTb, wkb], [pb])
                for ji in range(nj):
                    j = j0 + ji
                    for d in range(2):
                        c.act(kp[:, d, j, :], p[:, ji * 128:(ji + 1) * 128], AF.Copy, [pb, utokb], [kpb], scale=utok[:, j, d * 16 + h:d * 16 + h + 1])
            wt, wb = pf.next()

            def ev_bv(j0, nj, p, pb):
                c.cp("act", vaug[:, j0:j0 + nj, 0:128], p[:, 0:nj * 128].rearrange("p (j f) -> p j f", f=128), [pb], [vaugb])
            proj_tm(wt, wb, 1, h, ev_bv)
            wt, wb = pf.next()

            def ev_bo(j0, nj, p, pb):
                c.act(og[:, j0:j0 + nj, :].rearrange("p j f -> p (j f)"), p[:, 0:nj * 128], AF.Sigmoid, [pb], [ogb])
            proj_tm(wt, wb, 2, h, ev_bo)

            if h == 0:
                dump("caT", caT[:], [128, NT], caTb, bf=True)
                dump("qbT", qbT[:], [128, NT], qbTb, bf=True)
                dump("kbT", kbT[:], [128, NT], kbTb, bf=True)
                dump("kp", kp[:], [128, 2, NTILE, 128], kpb, bf=True)
                dump("vaug", vaug[:], [128, NTILE, 129], vaugb, bf=True)
                dump("og", og[:], [128, NTILE, 128], ogb, bf=True)
            S.mark(f"L{l} mlstm h{h} chain")
            for d in range(2):
                chs = [Chat[d], Chat2[d]]
                cur = 0
                cb, cbb = Cbf[d]
                order = list(range(NTILE)) if d == 0 else list(range(NTILE - 1, -1, -1))
                for g in range(NSEG):
                    tiles = order[g * 2:(g + 1) * 2]
                    p, pb = c.ps()
                    for i, j in enumerate(tiles):
                        c.mm(p[:, i * 129:i * 129 + 129], kp[:, d, j, :], vaug[:, j, :], True, True, [kpb, vaugb], [pb])
                    for i, j in enumerate(tiles):
                        seg = j // 2
                        entry = (i == 0)
                        final = (i == 1)
                        ch, chb = chs[cur]
                        if entry:
                            if seg == 4:
                                c.memset("dve", ch[:], 0.0, [chb])
                            elif (d == 0 and seg == 0) or (d == 1 and seg == 3):
                                c.dma("sp", ch[:, 0:128], cinit[l, d, h], W=[chb], sem=f"cin{d}")
                                c.dma("sp", ch[:, 128:129], ninit[l, d, h].rearrange("(k o) -> k o", o=1), W=[chb], sem=f"cin{d}")
                            else:
                                c.ts("dve", ch[:], ch[:], flag[:, 0:1], None, ALU.mult, None, [chb, flagb], [chb])
                        acol = abc[:, d, h, j:j + 1]
                        c.act(cb[:, j, :], ch[:], AF.Copy, [chb, abcb], [cbb], scale=acol)
                        nx, nxb = chs[1 - cur]
                        c.stt("dve", nx[:], ch[:], acol, p[:, i * 129:i * 129 + 129], ALU.mult, ALU.add, [chb, abcb, pb], [nxb])
                        cur = 1 - cur
                        if final:
                            sgt, sgtb = cstage[cst_i[0] % 2]
                            c.cp("pool", sgt[:], nx[:, 0:128], [nxb], [sgtb])
                            c.dma("sp", o_C[seg, l, d, h], sgt[:], R=[sgtb], sem=f"cstage{cst_i[0] % 2}", is_out=True)
                            cst_i[0] += 1
                            c.cp("pool", ncols[:, seg, d, h:h + 1], nx[:, 128:129], [nxb], [ncolsb])

            S.mark(f"L{l} mlstm h{h} out")
            for j in range(NTILE):
                for d in range(2):
                    psT, psTb = c.ps()
                    c.mm(psT[:, 0:128], kbT[:, j * 128:(j + 1) * 128], qbT[:, j * 128:(j + 1) * 128], True, True, [kbTb, qbTb], [psTb])
                    c.stt("dve", ssb[:], psT[:, 0:128], utok[:, j, d * 16 + h:d * 16 + h + 1], bmask[:, d, :], ALU.mult, ALU.mult, [psTb, utokb, bmaskb], [ssbb])
                    pn, pnb = c.ps()
                    c.mm(pn[:, 0:129], ssb[:], vaug[:, j, :], True, False, [ssbb, vaugb], [pnb])
                    c.mm(pn[:, 0:129], qbT[:, j * 128:(j + 1) * 128], Cbf[d][0][:, j, :], False, True, [qbTb, Cbf[d][1]], [pnb])
                    c.cp("dve", dn[:, 3:4], pn[:, 128:129], [pnb], [dnb])
                    c.stt("dve", dn[:, 0:1], dn[:, 3:4], -1.0, dn[:, 3:4], ALU.mult, ALU.max, [dnb], [dnb])
                    c.tt("dve", dn[:, 1:2], dn[:, 0:1], utok[:, j, d * 16 + 8 + h:d * 16 + 8 + h + 1], ALU.max, [dnb, utokb], [dnb])
                    S.op("dve", lambda e: e.reciprocal(out=dn[:, 2:3], in_=dn[:, 1:2]), reads=[dnb], writes=[dnb])
                    if d == 0:
                        c.act(hacc[:], pn[:, 0:128], AF.Copy, [pnb, dnb], [haccb], scale=dn[:, 2:3])
                    else:
                        c.stt("dve", hacc[:], pn[:, 0:128], dn[:, 2:3], hacc[:], ALU.mult, ALU.add, [pnb, dnb, haccb], [haccb])
                S.op("dve", lambda e: e.bn_stats(out=bst[:], in_=hacc[:]), reads=[haccb], writes=[bstb])
                S.op("dve", lambda e: e.bn_aggr(out=bmv[:, 0:2], in_=bst[:]), reads=[bstb], writes=[bmvb])
                c.act(bmv[:, 2:3], bmv[:, 1:2], AF.Sqrt, [bmvb, epsb], [bmvb], bias=eps5[:, 1:2])
                S.op("dve", lambda e: e.reciprocal(out=bmv[:, 3:4], in_=bmv[:, 2:3]), reads=[bmvb], writes=[bmvb])
                c.ts("dve", y2[:], hacc[:], bmv[:, 0:1], bmv[:, 3:4], ALU.subtract, ALU.mult, [haccb, bmvb], [y2b])
                c.tt("pool", y2[:], y2[:], mgbc[:], ALU.mult, [y2b, mgbcb], [y2b])
                c.tt("pool", ybk[:], y2[:], og[:, j, :], ALU.mult, [y2b, ogb], [ybkb])
                pt, ptb = c.ps()
                ptv = pt[:].bitcast(BF16)
                c.tr(ptv[:, 0:128], ybk[:], identh[:], [ybkb, identhb], [ptb])
                c.cp("act", ybT[:, h, j * 128:(j + 1) * 128], ptv[:, 0:128], [ptb], [ybTb[h]])

        dump("ybT", ybT[:], [128, 8, NT], ybTb, bf=True)
        nrow, nrowb = mlt("nrow", [80, 128])
        p, pb = c.ps()
        c.tr(p[0:80, 0:128], ncols[:].rearrange("p s d h -> p (s d h)"), ident[:], [ncolsb, identb], [pb])
        c.cp("dve", nrow[:], p[0:80, 0:128], [pb], [nrowb])
        for seg in range(NSEG):
            c.dma("sp", o_n[seg, l].rearrange("d h k -> (d h) k"), nrow[seg * 16:(seg + 1) * 16, :], R=[nrowb], sem="on", is_out=True)
        ml_es.close()
        S.barrier()

        S.mark(f"L{l} merge")
        mg_es = contextlib.ExitStack()

        def mgt(name, shape, dt=F32, mg_es=mg_es, l=l):
            t = mg_es.enter_context(nc.sbuf_tensor(f"L{l}c_{name}", shape, dt))
            return t, Buf(f"L{l}c_{name}")
        mrg = mg_es.enter_context(nc.sbuf_tensor(f"L{l}_mrg", [128, 8, NT], BF16))
        mrgb = [Buf(f"mrg{e}") for e in range(8)]
        mg1 = contextlib.ExitStack()

        def mg1t(name, shape, dt=F32, mg1=mg1, l=l):
            t = mg1.enter_context(nc.sbuf_tensor(f"L{l}d_{name}", shape, dt))
            return t, Buf(f"L{l}d_{name}")
        sga, sgab = mg1t("sga", [128, 512])
        sgb_, sgbb = mg1t("sgb", [128, 512])
        m1, m1b = mg1t("m1", [128, 512])
        m2, m2b = mg1t("m2", [128, 512])
        fns = []
        for e_ in range(8):
            fns += [w8fn(wl, 8224 + e_ * 128), w8fn(wl, 9248 + e_ * 128), w8fn(w_ba[l], e_ * 128), w8fn(w_bb[l], e_ * 128)]
        pf = Prefetch(r8, fns, depth=4)
        for e_ in range(8):
            wga, wgab = pf.next()
            wgb, wgbb = pf.next()
            wa, wab = pf.next()
            wb_, wbb = pf.next()
            for pi, (t0, n) in enumerate(PIECES):
                p, pb = c.ps()
                for k in range(8):
                    c.mm(p[:, 0:n], wga[:, k, :], hT[:, k, t0:t0 + n], k == 0, k == 7, [wgab, hTb], [pb])
                c.act(sga[:, 0:n], p[:, 0:n], AF.Sigmoid, [pb, bcolb], [sgab], bias=bcol[:, 64 + e_:65 + e_])
                p, pb = c.ps()
                for k in range(8):
                    c.mm(p[:, 0:n], wgb[:, k, :], hT[:, k, t0:t0 + n], k == 0, k == 7, [wgbb, hTb], [pb])
                c.act(sgb_[:, 0:n], p[:, 0:n], AF.Sigmoid, [pb, bcolb], [sgbb], bias=bcol[:, 72 + e_:73 + e_])
                p, pb = c.ps()
                for k in range(8):
                    c.mm(p[:, 0:n], wa[:, k, :], yaT[:, k, t0:t0 + n], k == 0, k == 7, [wab, yaTb[k]], [pb])
                c.tt("dve", m1[:, 0:n], p[:, 0:n], sga[:, 0:n], ALU.mult, [pb, sgab], [m1b])
                p, pb = c.ps()
                for k in range(8):
                    c.mm(p[:, 0:n], wb_[:, k, :], ybT[:, k, t0:t0 + n], k == 0, k == 7, [wbb, ybTb[k]], [pb])
                c.tt("dve", m2[:, 0:n], p[:, 0:n], sgb_[:, 0:n], ALU.mult, [pb, sgbb], [m2b])
                c.tt("pool", mrg[:, e_, t0:t0 + n], m1[:, 0:n], m2[:, 0:n], ALU.add, [m1b, m2b], [mrgb[e_]])
        dump("mrg", mrg[:], [128, 8, NT], mrgb, bf=True)
        mg1.close()
        S.barrier()

        S.mark(f"L{l} outproj+LN1")
        r2 = Ring(c, "w2_", [128, 4, D], 2)
        r2.slots = [mgt(f"w2s{i}", [128, 4, D], BF16) for i in range(2)]

        def wo_view(t):
            return t[:].rearrange("p a b -> p (a b)").rearrange("p (k n) -> p k n", n=512)
        wo = []
        for hf in range(2):
            t_, b_ = r2.load(lambda t, hf=hf: [(wo_view(t), w_out[l][:, hf * 512:(hf + 1) * 512].rearrange("(k p) n -> p k n", p=128))])
            wo.append((wo_view(t_), b_))
        lnv, lnvb = mgt("lnv", [128, 2, D])
        load_lnv(lnv, lnvb, 0)
        gbc, gbcb = mgt("gbc", [128, D])
        vt, vtb = mgt("vt", [128, D])
        stats, statsb = mgt("stats", [128, 2, 6])
        mv, mvb = mgt("mv", [128, 4])
        for j in range(NTILE):
            seg = j // 2
            if j % 2 == 0:
                gbc_build(0, seg, gbc, gbcb)
            for hf in range(2):
                p, pb = c.ps()
                for k in range(8):
                    c.mm(p[:, :], mrg[:, k, j * 128:(j + 1) * 128], wo[hf][0][:, k, :], k == 0, k == 7, [mrgb[k], wo[hf][1]], [pb])
                c.tt("dve", vt[:, hf * 512:(hf + 1) * 512], p[:, :], gbc[:, hf * 512:(hf + 1) * 512], ALU.mult, [pb, gbcb], [vtb])
            c.stt("dve", vt[:], xt[:, j, :], ALPHA, vt[:], ALU.mult, ALU.add, [xb[j], vtb], [vtb])
            layer_norm_tile(j, vt, vtb, lnv, lnvb, stats, statsb, mv, mvb)
        dump("x1", xt[:], [128, NTILE, D], xb)
        mg_es.close()
        mx.close()
        S.barrier()

        S.mark(f"L{l} ffn")
        transpose_mod(1)
        ff = contextlib.ExitStack()

        def ft(name, shape, dt=F32, ff=ff, l=l):
            t = ff.enter_context(nc.sbuf_tensor(f"L{l}f_{name}", shape, dt))
            return t, Buf(f"L{l}f_{name}")
        facc = ff.enter_context(nc.sbuf_tensor(f"L{l}_facc", [128, NTILE, D], F32))
        faccb = [Buf(f"facc{j}") for j in range(NTILE)]
        c.memset("pool", facc[:], 0.0, faccb)
        r2 = Ring(c, "w2_", [128, 4, D], 2)
        r2.slots = [ft(f"w2s{i}", [128, 4, D], BF16) for i in range(2)]
        gT = [ft(f"gT{i}", [128, 4, NT], BF16) for i in range(2)]
        s1, s1b = ft("s1", [128, 512], BF16)
        comb, combb = ft("comb", [128, NTILE, NE])
        jdx = l // 2
        moe = (l % 2 == 1)
        if moe:
            rw, rwb = ft("rw", [128, 8, NE], BF16)
            c.dma("pool", rw[:], r_w[jdx].rearrange("(k p) n -> p k n", p=128), W=[rwb], sem="rw")
            rbr, rbrb = ft("rbr", [1, NE])
            c.dma("sp", rbr[:], r_b[jdx:jdx + 1, :], W=[rbrb])
            lgt, lgtb = ft("lgt", [128, NTILE, NE])
            p, pb = c.ps()
            for j in range(NTILE):
                o = p[:, j * NE:(j + 1) * NE]
                for k in range(8):
                    c.mm(o, hT[:, k, j * 128:(j + 1) * 128], rw[:, k, :], k == 0, False, [hTb, rwb], [pb])
                c.mm(o, ones32[0:1, 0:128], rbr[0:1, :], False, True, [ones32b, rbrb], [pb])
            c.cp("dve", lgt[:].rearrange("p j e -> p (j e)"), p[:, 0:NTILE * NE], [pb], [lgtb])
            m1_, m1_b = ft("rm1", [128, NTILE, 1])
            m2_, m2_b = ft("rm2", [128, NTILE, 1])
            tmpr, tmprb = ft("tmpr", [128, NTILE, NE])
            msk, mskb = ft("msk", [128, NTILE, NE])
            S.op("dve", lambda e: e.tensor_reduce(out=m1_[:].rearrange("p j o -> p (j o)"), in_=lgt[:], axis=AX.X, op=ALU.max), reads=[lgtb], writes=[m1_b])
            bshape = [128, NTILE, NE]
            c.tt("dve", msk[:], lgt[:], m1_[:].to_broadcast(bshape), ALU.is_equal, [lgtb, m1_b], [mskb])
            c.stt("dve", tmpr[:], msk[:], -1e30, lgt[:], ALU.mult, ALU.add, [mskb, lgtb], [tmprb])
            S.op("dve", lambda e: e.tensor_reduce(out=m2_[:].rearrange("p j o -> p (j o)"), in_=tmpr[:], axis=AX.X, op=ALU.max), reads=[tmprb], writes=[m2_b])
            c.tt("dve", msk[:], lgt[:], m2_[:].to_broadcast(bshape), ALU.is_ge, [lgtb, m2_b], [mskb])
            c.tt("dve", tmpr[:], lgt[:], m1_[:].to_broadcast(bshape), ALU.subtract, [lgtb, m1_b], [tmprb])
            c.act(tmpr[:], tmpr[:], AF.Exp, [tmprb], [tmprb])
            c.tt("dve", tmpr[:], tmpr[:], msk[:], ALU.mult, [tmprb, mskb], [tmprb])
            S.op("dve", lambda e: e.tensor_reduce(out=m2_[:].rearrange("p j o -> p (j o)"), in_=tmpr[:], axis=AX.X, op=ALU.add), reads=[tmprb], writes=[m2_b])
            S.op("dve", lambda e: e.reciprocal(out=m2_[:], in_=m2_[:]), reads=[m2_b], writes=[m2_b])
            c.tt("dve", comb[:], tmpr[:], m2_[:].to_broadcast(bshape), ALU.mult, [tmprb, m2_b], [combb])
            experts = [(moe_w1[jdx, e_], moe_w3[jdx, e_], moe_w2[jdx, e_], e_) for e_ in range(NE)]
        else:
            experts = [(ffn_w1[jdx], ffn_w3[jdx], ffn_w2[jdx], None)]

        fns13 = []
        fns2 = []
        for (w1, w3, w2, ei) in experts:
            for (f0, f1) in FGROUPS:
                for f in range(f0, f1):
                    fns13 += [w8fn(w1, f * 128), w8fn(w3, f * 128)]
                fns2.append(lambda t, w2=w2, f0=f0, f1=f1: [(t[:, 0:f1 - f0, :], w2[f0 * 128:f1 * 128, :].rearrange("(f p) n -> p f n", p=128))])
        pf13 = Prefetch(r8, fns13)
        pf2 = Prefetch(r2, fns2, depth=2)
        gi_ = 0
        for (w1, w3, w2, ei) in experts:
            for (f0, f1) in FGROUPS:
                g_, g_b = gT[gi_ % 2]
                gi_ += 1
                for fi, f in enumerate(range(f0, f1)):
                    w1t, w1b = pf13.next()
                    w3t, w3b = pf13.next()
                    for pi, (t0, n) in enumerate(PIECES):
                        p1, p1b = c.ps()
                        for k in range(8):
                            c.mm(p1[:, 0:n], w1t[:, k, :], hT[:, k, t0:t0 + n], k == 0, k == 7, [w1b, hTb], [p1b])
                        p3, p3b = c.ps()
                        for k in range(8):
                            c.mm(p3[:, 0:n], w3t[:, k, :], hT[:, k, t0:t0 + n], k == 0, k == 7, [w3b, hTb], [p3b])
                        c.act(s1[:, 0:n], p1[:, 0:n], AF.Silu, [p1b], [s1b])
                        c.tt("dve", g_[:, fi, t0:t0 + n], p3[:, 0:n], s1[:, 0:n], ALU.mult, [p3b, s1b], [g_b])
                w2t, w2b = pf2.next()
                for j in range(NTILE):
                    for hf in range(2):
                        p, pb = c.ps()
                        nf = f1 - f0
                        for fi in range(nf):
                            c.mm(p[:, :], g_[:, fi, j * 128:(j + 1) * 128], w2t[:, fi, hf * 512:(hf + 1) * 512], fi == 0, fi == nf - 1, [g_b, w2b], [pb])
                        sc = 1.0 if ei is None else comb[:, j, ei:ei + 1]
                        rd = [pb, faccb[j]] + ([] if ei is None else [combb])
                        c.stt("dve", facc[:, j, hf * 512:(hf + 1) * 512], p[:, :], sc, facc[:, j, hf * 512:(hf + 1) * 512], ALU.mult, ALU.add, rd, [faccb[j]])
        dump("facc", facc[:], [128, NTILE, D], faccb)
        S.mark(f"L{l} LN2")
        lnv, lnvb = ft("lnv2", [128, 2, D])
        load_lnv(lnv, lnvb, 1)
        gbc, gbcb = ft("gbc2", [128, D])
        stats, statsb = ft("stats2", [128, 2, 6])
        mv, mvb = ft("mv2", [128, 4])
        for j in range(NTILE):
            seg = j // 2
            if j % 2 == 0:
                gbc_build(1, seg, gbc, gbcb)
            c.tt("dve", facc[:, j, :], facc[:, j, :], gbc[:], ALU.mult, [faccb[j], gbcb], [faccb[j]])
            c.stt("dve", facc[:, j, :], xt[:, j, :], ALPHA, facc[:, j, :], ALU.mult, ALU.add, [xb[j], faccb[j]], [faccb[j]])
            layer_norm_tile(j, facc[:, j, :], faccb[j], lnv, lnvb, stats, statsb, mv, mvb)
        ff.close()
        es.close()
        S.barrier()

    c.dma("sp", yout.rearrange("(j p) d -> p j d", p=128), xt[:], R=xb, sem="yout", is_out=True)
    S.emit()
    c.st.close()
    print("n semaphores:", len(S.semnames))
    S.mark("end")
    nc._dbg_names = list(dbg.keys())
    nc._marks = S.marks
    return nc


def _consts():
    ident = np.eye(128, dtype=np.float32)
    s = np.arange(128)[:, None]
    t = np.arange(128)[None, :]
    same64 = (s // 64) == (t // 64)
    same32 = (s // 32) == (t // 32)
    amask = np.stack([(same32 & (s <= t)), (same32 & (s >= t)), (same64 & ((s // 32) < (t // 32))), (same64 & ((s // 32) > (t // 32)))], axis=1).astype(np.float32)
    bmask = np.stack([(s <= t), (s >= t)], axis=1).astype(np.float32)
    tt = np.arange(NT)
    smask = np.ascontiguousarray(np.broadcast_to((tt % 64 != 0), (128, NT))).astype(np.float32)
    sel = np.zeros((NSEG, NSEG, 128), np.float32)
    for k in range(NSEG):
        sel[k, k, :] = 1.0
    return ident, amask, bmask, smask, sel


def _cmask(is_sample):
    m = np.zeros((2, 128, 1154), np.float32)
    t = np.arange(1024)
    per = 64 if is_sample else 256
    m[0, :, PADA:PADA + 1024] = (t % per != per - 1)
    m[1, :, PADA:PADA + 1024] = (t % per != 0)
    return m


_NC_CACHE = {}


def kernel(_cores=None, _dbgout=None, **inp):
    inp = {k: np.ascontiguousarray(np.asarray(v)) for k, v in inp.items()}
    if "nc" not in _NC_CACHE:
        _NC_CACHE["nc"] = build()
    nc = _NC_CACHE["nc"]
    ident, amask, bmask, smask, sel = _consts()
    xp, xs = inp["x_prompt"], inp["x_sample"]
    wnames = ["w_ada", "b_ada", "w_in", "b_in", "hgrn_lb_raw", "hgrn_norm_g", "conv_w", "conv_b", "w_mq", "w_mk", "mlstm_fbias",
              "mlstm_norm_g", "w_branch_a", "w_branch_b", "w_out", "ln1_g", "ln1_b", "ln2_g", "ln2_b", "ffn_w1", "ffn_w3", "ffn_w2",
              "moe_router_w", "moe_router_b", "moe_w1", "moe_w3", "moe_w2"]
    in_maps = []
    segmap = []
    for cid in range(NCORES):
        if cid < 2:
            prompts = [cid]
            x = np.concatenate([xs[cid], xp[cid]], axis=0)
            cond = np.concatenate([np.repeat(inp["c"][cid:cid + 1], 4, axis=0), inp["c_ctx"][None]], axis=0)
            smp = True
            hinit = inp["state_hgrn"][cid]
            cinit = inp["state_mlstm_C"][cid]
            ninit = inp["state_mlstm_n"][cid]
            minit = inp["state_mlstm_m"][cid]
            segmap.append([None, None, None, None, cid])
        else:
            prompts = [2 + (cid - 2) * 5 + j for j in range(5)]
            x = np.concatenate([xp[p] for p in prompts], axis=0)
            cond = np.repeat(inp["c_ctx"][None], 5, axis=0)
            smp = False
            hinit = np.zeros((DEPTH, 2, 8, 128, 128), np.float32)
            cinit = np.zeros((DEPTH, 2, 8, 128, 128), np.float32)
            ninit = np.zeros((DEPTH, 2, 8, 128), np.float32)
            minit = np.zeros((DEPTH, 2, 8), np.float32)
            segmap.append(prompts)
        m = {"xin": np.ascontiguousarray(x), "cond": np.ascontiguousarray(cond),
             "flag": np.full((128, 1), 1.0 if smp else 0.0, np.float32), "cmask": _cmask(smp),
             "hinit": np.ascontiguousarray(hinit), "cinit": np.ascontiguousarray(cinit),
             "ninit": np.ascontiguousarray(ninit), "minit": np.ascontiguousarray(minit),
             "ident": ident, "amask": amask, "bmask": bmask, "smask": smask, "sel": sel}
        for w in wnames:
            m[w] = inp[w]
        in_maps.append(m)
    if _cores is not None:
        res = run_bass_kernel_spmd(nc, [in_maps[i] for i in _cores], core_ids=list(range(len(_cores))))
        if _dbgout is not None:
            for ci, cid in enumerate(_cores):
                _dbgout[cid] = res.results[ci]
        return None
    res = run_bass_kernel_spmd(nc, in_maps, core_ids=list(range(NCORES)))
    R = res.results
    y_prompt = np.zeros((32, 256, D), np.float32)
    y_sample = np.zeros((2, 1024, D), np.float32)
    nh = np.zeros((32, DEPTH, 2, 8, 128, 128), np.float32)
    nC = np.zeros((32, DEPTH, 2, 8, 128, 128), np.float32)
    nn = np.zeros((32, DEPTH, 2, 8, 128), np.float32)
    nm = np.zeros((32, DEPTH, 2, 8), np.float32)
    for cid in range(NCORES):
        r = R[cid]
        if cid < 2:
            y_sample[cid] = r["yout"][0:1024]
        for seg, p in enumerate(segmap[cid]):
            if p is None:
                continue
            y_prompt[p] = r["yout"][seg * 256:(seg + 1) * 256]
            nh[p] = r["o_h"][seg]
            nC[p] = r["o_C"][seg]
            nn[p] = r["o_n"][seg]
            nm[p] = r["o_m"][seg]
    return (y_prompt, y_sample, nh, nC, nn, nm)
```

```python
import contextlib
import numpy as np
import concourse.bass as bass
import concourse.mybir as mybir
from concourse.bass_utils import run_bass_kernel_spmd

F32 = mybir.dt.float32
BF16 = mybir.dt.bfloat16
AF = mybir.ActivationFunctionType
ALU = mybir.AluOpType
AX = mybir.AxisListType

NCORES = 8
DEPTH = 4
D = 1024
NT = 1280
NTILE = 10
NSEG = 5
DFF = 2816
NF = 22
NE = 8
N_IN = 10272
ALPHA = (2 * DEPTH) ** 0.25
PIECES = [(0, 512), (512, 512), (1024, 256)]
FGROUPS = [(0, 4), (4, 8), (8, 12), (12, 16), (16, 19), (19, 22)]
PADA = 65


class Buf:
    __slots__ = ("name", "w", "r", "psum")

    def __init__(self, name, psum=False):
        self.name = name
        self.w = None
        self.r = []
        self.psum = psum


class Sched:
    EPOCH = 20000

    def __init__(self, nc):
        self.nc = nc
        self.eng = {"pe": nc.tensor, "dve": nc.vector, "act": nc.scalar, "pool": nc.gpsimd, "sp": nc.sync}
        self.prog = {k: [] for k in self.eng}
        self.cnt = {k: 0 for k in self.eng}
        self.epoch = {k: 0 for k in self.eng}
        self.seen = {k: {} for k in self.eng}
        self.semnames = []
        self.semobj = {}
        self.dmacnt = {}
        self.out_dma = []
        self.fence = []
        self.marks = []

    def mark(self, name):
        self.marks.append((name, {e: len(self.prog[e]) for e in self.eng}))

    def barrier(self):
        f = []
        for e in self.eng:
            if self.cnt[e] > 0:
                f.append((("E", e, self.epoch[e]), self.cnt[e]))
        for key, v in self.dmacnt.items():
            f.append((key, v))
        self.fence = f

    def _semkey(self, key):
        if key not in self.semobj:
            self.semobj[key] = None
            self.semnames.append(key)
        return key

    def _deps(self, e, reads, writes):
        deps = list(self.fence)
        for b in reads:
            if b.w is not None:
                deps.append(b.w)
            if b.psum:
                deps.extend(b.r)
        for b in writes:
            if b.w is not None:
                deps.append(b.w)
            deps.extend(b.r)
        need = {}
        seen = self.seen[e]
        for (k, v) in deps:
            if k[0] == "E" and k[1] == e and e == "pe":
                continue
            if seen.get(k, 0) >= v:
                continue
            if need.get(k, 0) < v:
                need[k] = v
        for k, v in need.items():
            seen[k] = v
        return list(need.items())

    def op(self, e, fn, reads=(), writes=()):
        waits = self._deps(e, reads, writes)
        if self.cnt[e] >= self.EPOCH:
            self.epoch[e] += 1
            self.cnt[e] = 0
        key = self._semkey(("E", e, self.epoch[e]))
        self.cnt[e] += 1
        tok = (key, self.cnt[e])
        self.prog[e].append((waits, fn, key, 1))
        for b in writes:
            b.w = tok
            b.r = []
        for b in reads:
            if b not in writes:
                if b.psum:
                    b.w = tok
                    b.r = []
                else:
                    b.r.append(tok)
        return tok

    def dma(self, q, out_ap, in_ap, reads=(), writes=(), semkey=None, is_out=False, **kw):
        waits = self._deps(q, reads, writes)
        key = self._semkey(("D", semkey))
        self.dmacnt[key] = self.dmacnt.get(key, 0) + 16
        tok = (key, self.dmacnt[key])

        kw = dict(kw)
        kw.setdefault("allow_slow_non_contiguous", True)

        def fn(eng, out_ap=out_ap, in_ap=in_ap, kw=kw):
            return eng.dma_start(out=out_ap, in_=in_ap, **kw)
        self.prog[q].append((waits, fn, key, 16))
        for b in writes:
            b.w = tok
            b.r = []
        for b in reads:
            b.r.append(tok)
        if is_out:
            self.out_dma.append(tok)
        return tok

    def emit(self):
        nc = self.nc
        fin = {}
        for (k, v) in self.out_dma:
            fin[k] = max(fin.get(k, 0), v)
        with contextlib.ExitStack() as st:
            for key in self.semnames:
                nm = "s_" + "_".join(str(x) for x in key)
                self.semobj[key] = st.enter_context(nc.semaphore(nm))
            so = self.semobj
            prog = self.prog
            for key in self.semnames:
                nc.gpsimd.sem_clear(so[key])
            nc.all_engine_barrier()

            def run(e, eng):
                for (waits, fn, key, inc) in prog[e]:
                    for (k, v) in waits:
                        eng.wait_ge(so[k], v)
                    fn(eng).then_inc(so[key], inc)

            with nc.Block() as block:
                @block.tensor
                def _(eng):
                    run("pe", eng)

                @block.vector
                def _(eng):
                    run("dve", eng)

                @block.scalar
                def _(eng):
                    run("act", eng)

                @block.gpsimd
                def _(eng):
                    run("pool", eng)

                @block.sync
                def _(eng):
                    run("sp", eng)
                    for k, v in fin.items():
                        eng.wait_ge(so[k], v)
            nc.all_engine_barrier()
            for key in self.semnames:
                nc.gpsimd.sem_clear(so[key])
            nc.all_engine_barrier()


class Ctx:
    def __init__(self, nc):
        self.nc = nc
        self.S = Sched(nc)
        self.st = contextlib.ExitStack()
        self.nps = 0
        self.pbanks = []
        self.reserved = set()
        self.uid = 0

    def sb(self, name, shape, dt=F32):
        t = self.st.enter_context(self.nc.sbuf_tensor("t_" + name, shape, dt))
        return t

    def tile(self, name, shape, dt=F32):
        return self.sb(name, shape, dt), Buf(name)

    def init_psum(self):
        for i in range(8):
            t = self.st.enter_context(self.nc.psum_tensor(f"pb{i}", [128, 512], F32))
            self.pbanks.append((t, Buf(f"pb{i}", psum=True)))

    def ps(self):
        while True:
            i = self.nps % 8
            self.nps += 1
            if i not in self.reserved:
                return self.pbanks[i]

    def ps_reserve(self, n):
        out = []
        for _ in range(n):
            while True:
                i = self.nps % 8
                self.nps += 1
                if i not in self.reserved:
                    break
            self.reserved.add(i)
            out.append(self.pbanks[i])
        return out

    def ps_release(self):
        self.reserved = set()

    def mm(self, out, lhsT, rhs, start, stop, R, W):
        self.S.op("pe", lambda e: e.matmul(out, lhsT=lhsT, rhs=rhs, start=start, stop=stop), reads=R, writes=W)

    def tr(self, out, in_, ident, R, W):
        self.S.op("pe", lambda e: e.transpose(out, in_, ident), reads=R, writes=W)

    def act(self, out, in_, func, R, W, bias=None, scale=1.0):
        if bias is None:
            self.S.op("act", lambda e: e.activation(out=out, in_=in_, func=func, scale=scale), reads=R, writes=W)
        else:
            self.S.op("act", lambda e: e.activation(out=out, in_=in_, func=func, bias=bias, scale=scale), reads=R, writes=W)

    def tt(self, eng, out, in0, in1, op, R, W):
        self.S.op(eng, lambda e: e.tensor_tensor(out=out, in0=in0, in1=in1, op=op), reads=R, writes=W)

    def ts(self, eng, out, in0, s1, s2, op0, op1, R, W):
        if s2 is None:
            self.S.op(eng, lambda e: e.tensor_scalar(out=out, in0=in0, scalar1=s1, scalar2=None, op0=op0), reads=R, writes=W)
        else:
            self.S.op(eng, lambda e: e.tensor_scalar(out=out, in0=in0, scalar1=s1, scalar2=s2, op0=op0, op1=op1), reads=R, writes=W)

    def stt(self, eng, out, in0, scalar, in1, op0, op1, R, W):
        self.S.op(eng, lambda e: e.scalar_tensor_tensor(out=out, in0=in0, scalar=scalar, in1=in1, op0=op0, op1=op1), reads=R, writes=W)

    def cp(self, eng, out, in_, R, W):
        if eng == "act":
            self.S.op("act", lambda e: e.copy(out=out, in_=in_), reads=R, writes=W)
        else:
            self.S.op(eng, lambda e: e.tensor_copy(out=out, in_=in_), reads=R, writes=W)

    def memset(self, eng, ap, val, W):
        self.S.op(eng, lambda e: e.memset(ap, val), writes=W)

    def dma(self, q, out, in_, R=(), W=(), sem=None, is_out=False, **kw):
        if sem is None:
            self.uid += 1
            sem = f"a{self.uid}_{q}"
        self.S.dma(q, out, in_, reads=R, writes=W, semkey=sem, is_out=is_out, **kw)


class Ring:
    def __init__(self, c, name, shape, n, dt=BF16, q="pool"):
        self.c = c
        self.slots = [c.tile(f"{name}{i}", shape, dt) for i in range(n)] if name != "w2_" else []
        self.n = n
        self.i = 0
        self.q = q
        self.name = name
        self.pending = []

    def load(self, fn):
        t, b = self.slots[self.i % self.n]
        sem = f"{self.name}{self.i % self.n}"
        self.i += 1
        pairs = fn(t)
        for (o, i_) in pairs:
            self.c.dma(self.q, o, i_, W=[b], sem=sem)
        return t, b


class Prefetch:
    def __init__(self, ring, fns, depth=None):
        self.ring = ring
        self.fns = fns
        self.depth = depth if depth is not None else max(1, ring.n - 1)
        self.loaded = []
        self.k = 0

    def _fill(self):
        while len(self.loaded) < len(self.fns) and len(self.loaded) < self.k + self.depth:
            self.loaded.append(self.ring.load(self.fns[len(self.loaded)]))

    def next(self):
        self._fill()
        r = self.loaded[self.k]
        self.k += 1
        return r


def build(nlayers=DEPTH, debug=False):
    nc = bass.Bass("TRN2", target_bir_lowering=False)
    c = Ctx(nc)
    S = c.S

    def din(name, shape):
        return nc.dram_tensor(name, shape, F32, kind="ExternalInput").ap()

    def dout(name, shape):
        return nc.dram_tensor(name, shape, F32, kind="ExternalOutput").ap()

    xin = din("xin", [NT, D])
    cond = din("cond", [NSEG, D])
    flag_d = din("flag", [128, 1])
    cmask_d = din("cmask", [2, 128, 1154])
    hinit = din("hinit", [DEPTH, 2, 8, 128, 128])
    cinit = din("cinit", [DEPTH, 2, 8, 128, 128])
    ninit = din("ninit", [DEPTH, 2, 8, 128])
    minit = din("minit", [DEPTH, 2, 8])
    ident_d = din("ident", [128, 128])
    amask_d = din("amask", [128, 4, 128])
    bmask_d = din("bmask", [128, 2, 128])
    smask_d = din("smask", [128, NT])
    sel_d = din("sel", [NSEG, NSEG, 128])
    w_ada = din("w_ada", [DEPTH, D, 6 * D])
    b_ada = din("b_ada", [DEPTH, 6 * D])
    w_in = din("w_in", [DEPTH, D, N_IN])
    b_in = din("b_in", [DEPTH, N_IN])
    lb_raw = din("hgrn_lb_raw", [2, DEPTH, D])
    hg_d = din("hgrn_norm_g", [DEPTH, 128])
    conv_w = din("conv_w", [DEPTH, 3, 3, D])
    conv_b = din("conv_b", [DEPTH, D])
    w_mq = din("w_mq", [DEPTH, 8, 128, 128])
    w_mk = din("w_mk", [DEPTH, 8, 128, 128])
    fbias = din("mlstm_fbias", [DEPTH, 2, 8])
    mg_d = din("mlstm_norm_g", [DEPTH, 128])
    w_ba = din("w_branch_a", [DEPTH, D, D])
    w_bb = din("w_branch_b", [DEPTH, D, D])
    w_out = din("w_out", [DEPTH, D, D])
    ln1_g = din("ln1_g", [DEPTH, D])
    ln1_b = din("ln1_b", [DEPTH, D])
    ln2_g = din("ln2_g", [DEPTH, D])
    ln2_b = din("ln2_b", [DEPTH, D])
    ffn_w1 = din("ffn_w1", [2, D, DFF])
    ffn_w3 = din("ffn_w3", [2, D, DFF])
    ffn_w2 = din("ffn_w2", [2, DFF, D])
    r_w = din("moe_router_w", [2, D, NE])
    r_b = din("moe_router_b", [2, NE])
    moe_w1 = din("moe_w1", [2, NE, D, DFF])
    moe_w3 = din("moe_w3", [2, NE, D, DFF])
    moe_w2 = din("moe_w2", [2, NE, DFF, D])

    yout = dout("yout", [NT, D])
    o_h = dout("o_h", [NSEG, DEPTH, 2, 8, 128, 128])
    o_C = dout("o_C", [NSEG, DEPTH, 2, 8, 128, 128])
    o_n = dout("o_n", [NSEG, DEPTH, 2, 8, 128])
    o_m = dout("o_m", [NSEG, DEPTH, 2, 8])

    c.init_psum()

    dbg = {}

    def dump(name, ap, shape, buf, bf=False):
        if not debug or name in dbg:
            return
        dt_ = nc.dram_tensor("dbg_" + name, list(shape), F32, kind="ExternalOutput").ap()
        dbg[name] = dt_
        bufs = buf if isinstance(buf, (list, tuple)) else [buf]
        c.dma("pool" if bf else "sp", dt_, ap, R=list(bufs), sem="dbg_" + name, is_out=True)

    xt = c.sb("xt", [128, NTILE, D])
    xb = [Buf(f"x{j}") for j in range(NTILE)]
    hT, hTb = c.tile("hT", [128, 8, NT], BF16)
    ident, identb = c.tile("ident", [128, 128])
    identh, identhb = c.tile("identh", [128, 128], BF16)
    ones32, ones32b = c.tile("ones32", [65, 128])
    onesf8, onesf8b = c.tile("onesf8", [8, 128])
    onesbf, onesbfb = c.tile("onesbf", [128, 128], BF16)
    flag, flagb = c.tile("flag", [128, 1])
    amask, amaskb = c.tile("amask", [128, 4, 128])
    bmask, bmaskb = c.tile("bmask", [128, 2, 128])
    smask, smaskb = c.tile("smask", [128, NT])
    sel, selb = c.tile("sel", [NSEG, NSEG, 128])
    actT, actTb = c.tile("actT", [128, 8, NSEG], BF16)
    lbt, lbtb = c.tile("lbt", [128, 2, DEPTH, 8])
    omlt, omltb = c.tile("omlt", [128, 2, DEPTH, 8])
    cmask = c.sb("cmask", [128, 2, 1154], BF16)
    cmaskb = Buf("cmask")

    for (t, b, src) in [(ident, identb, ident_d), (flag, flagb, flag_d), (amask, amaskb, amask_d),
                        (bmask, bmaskb, bmask_d), (sel, selb, sel_d)]:
        c.dma("sp", t[:], src, W=[b])
    c.dma("pool", identh[:], ident_d, W=[identhb])
    c.dma("sp", smask[:], smask_d, W=[smaskb])
    for i in range(2):
        c.dma("pool", cmask[:, i, :], cmask_d[i], W=[cmaskb], sem="cmask")
    c.memset("pool", ones32[:], 1.0, [ones32b])
    c.memset("pool", onesf8[:], 1.0, [onesf8b])
    c.memset("pool", onesbf[:], 1.0, [onesbfb])
    c.dma("sp", xt[:], xin.rearrange("(j p) d -> p j d", p=128), W=xb)

    with contextlib.ExitStack() as es:
        ct = es.enter_context(nc.sbuf_tensor("condt", [NSEG, D], F32)); ctb = Buf("condt")
        ca = es.enter_context(nc.sbuf_tensor("conda", [NSEG, D], F32)); cab = Buf("conda")
        c.dma("sp", ct[:], cond, W=[ctb])
        c.act(ca[:], ct[:], AF.Silu, [ctb], [cab])
        p, pb = c.ps()
        for k in range(8):
            c.tr(p[:, k * NSEG:(k + 1) * NSEG], ca[:, k * 128:(k + 1) * 128], ident[0:NSEG, 0:NSEG], [cab, identb], [pb])
        c.cp("dve", actT[:].rearrange("p k s -> p (k s)"), p[:, 0:8 * NSEG], [pb], [actTb])
        dump("ct", ct[:], [NSEG, D], ctb)
        dump("ca", ca[:], [NSEG, D], cab)
        dump("actT", actT[:], [128, 8, NSEG], actTb, bf=True)

        lr = es.enter_context(nc.sbuf_tensor("lbraw", [64, 128], F32)); lrb = Buf("lbraw")
        c.dma("sp", lr[:], lb_raw.rearrange("d l (c f) -> (d l c) f", f=128), W=[lrb])
        p, pb = c.ps()
        c.tr(p[:, 0:64], lr[:], ident[0:64, 0:64], [lrb, identb], [pb])
        ex = es.enter_context(nc.sbuf_tensor("lbex", [128, 2, DEPTH, 8], F32)); exb = Buf("lbex")
        c.act(ex[:].rearrange("p d l c -> p (d l c)"), p[:, 0:64], AF.Exp, [pb], [exb])
        sm = es.enter_context(nc.sbuf_tensor("lbsum", [128, 2, 8], F32)); smb = Buf("lbsum")
        c.tt("dve", sm[:], ex[:, :, 0, :], ex[:, :, 1, :], ALU.add, [exb], [smb])
        c.tt("dve", sm[:], sm[:], ex[:, :, 2, :], ALU.add, [exb, smb], [smb])
        c.tt("dve", sm[:], sm[:], ex[:, :, 3, :], ALU.add, [exb, smb], [smb])
        c.S.op("dve", lambda e: e.reciprocal(out=sm[:], in_=sm[:]), reads=[smb], writes=[smb])
        for l in range(DEPTH):
            c.tt("dve", ex[:, :, l, :], ex[:, :, l, :], sm[:], ALU.mult, [exb, smb], [exb])
        c.memset("dve", lbt[:, :, 0, :], 0.0, [lbtb])
        c.cp("dve", lbt[:, :, 1, :], ex[:, :, 1, :], [exb], [lbtb])
        c.tt("dve", lbt[:, :, 2, :], lbt[:, :, 1, :], ex[:, :, 2, :], ALU.add, [exb, lbtb], [lbtb])
        c.tt("dve", lbt[:, :, 3, :], lbt[:, :, 2, :], ex[:, :, 3, :], ALU.add, [exb, lbtb], [lbtb])
        c.ts("dve", omlt[:], lbt[:], -1.0, 1.0, ALU.mult, ALU.add, [lbtb], [omltb])

    S.barrier()
    r8 = Ring(c, "w8_", [128, 8, 128], 8)
    rsm = Ring(c, "wsm_", [128, 128], 4)

    def w8fn(src2d, c0, ncol=128):
        def fn(t):
            return [(t[:, :, 0:ncol], src2d[:, c0:c0 + ncol].rearrange("(k p) n -> p k n", p=128))]
        return fn

    print("SBUF after persistent:", nc.sbuf_bytes_remaining)

    for l in range(nlayers):
        c.uid = 100
        es = contextlib.ExitStack()

        def lt(name, shape, dt=F32, es=es, l=l):
            t = es.enter_context(nc.sbuf_tensor(f"L{l}_{name}", shape, dt))
            return t, Buf(f"L{l}_{name}")

        S.mark(f"L{l} params")
        bcol, bcolb = lt("bcol", [128, 80])
        brow, browb = lt("brow", [65, D])
        gbi, gbib = lt("gbi", [8, 4])
        cvw, cvwb = lt("cvw", [128, 9, 8])
        cvb, cvbb = lt("cvb", [128, 8])
        hg, hgb = lt("hg", [128, 1])
        mgbc, mgbcb = lt("mgbc", [128, 128])
        adab, adabb = lt("adab", [128, 48])
        modT, modTb = lt("modT", [128, 48, NSEG])
        gtok, gtokb = lt("gtok", [NSEG, 2, D])
        fbt, fbtb = lt("fbt", [8, 2])

        with contextlib.ExitStack() as e2:
            raw, rawb = (e2.enter_context(nc.sbuf_tensor(f"L{l}_raw", [80, 128], F32)), Buf("raw"))
            c.dma("sp", raw[0:64, :], b_in[l, 0:8192].rearrange("(c f) -> c f", f=128), W=[rawb], sem="raw")
            c.dma("sp", raw[64:80, :], b_in[l, 8224:N_IN].rearrange("(c f) -> c f", f=128), W=[rawb], sem="raw")
            p, pb = c.ps()
            c.tr(p[:, 0:80], raw[:], ident[0:80, 0:80], [rawb, identb], [pb])
            c.cp("dve", bcol[:], p[:, 0:80], [pb], [bcolb])
            raw2, raw2b = (e2.enter_context(nc.sbuf_tensor(f"L{l}_raw2", [80, 128], F32)), Buf("raw2"))
            c.dma("sp", raw2[0:72, :], conv_w[l].rearrange("a b (c f) -> (a b c) f", f=128), W=[raw2b], sem="raw2")
            c.dma("sp", raw2[72:80, :], conv_b[l].rearrange("(c f) -> c f", f=128), W=[raw2b], sem="raw2")
            p, pb = c.ps()
            c.tr(p[:, 0:80], raw2[:], ident[0:80, 0:80], [raw2b, identb], [pb])
            c.cp("dve", cvw[:].rearrange("p t c -> p (t c)"), p[:, 0:72], [pb], [cvwb])
            c.cp("dve", cvb[:], p[:, 72:80], [pb], [cvbb])
            raw3, raw3b = (e2.enter_context(nc.sbuf_tensor(f"L{l}_raw3", [48, 128], F32)), Buf("raw3"))
            c.dma("sp", raw3[:], b_ada[l].rearrange("(c f) -> c f", f=128), W=[raw3b])
            p, pb = c.ps()
            c.tr(p[:, 0:48], raw3[:], ident[0:48, 0:48], [raw3b, identb], [pb])
            c.cp("dve", adab[:], p[:, 0:48], [pb], [adabb])
        S.barrier()
        for gi, off in enumerate([3072, 6144, 7168]):
            c.dma("sp", brow[32 * gi:32 * gi + 1, :], b_in[l:l + 1, off:off + D], W=[browb], sem="brow")
        with nc.allow_non_contiguous_dma(reason="tiny param loads"):
            c.dma("sp", gbi[:], b_in[l, 8192:8224].rearrange("(g h) -> h g", h=8), W=[gbib])
            c.dma("sp", hg[:], hg_d[l].rearrange("(f o) -> f o", o=1), W=[hgb])
            c.dma("sp", fbt[:], fbias[l].rearrange("d h -> h d"), W=[fbtb])
        c.dma("sp", mgbc[:], mg_d[l:l + 1, :].partition_broadcast(128), W=[mgbcb])
        gb_es = contextlib.ExitStack()
        gbrow = gb_es.enter_context(nc.sbuf_tensor(f"L{l}_gbrow", [NSEG, 2, D], F32)); gbrowb = Buf("gbrow")
        for i, off in enumerate([2048, 5120]):
            c.dma("sp", gbrow[:, i, :], b_ada[l:l + 1, off:off + D].partition_broadcast(NSEG), W=[gbrowb], sem="gbrow")

        S.mark(f"L{l} adaln")
        pf = Prefetch(r8, [w8fn(w_ada[l], n * 128) for n in range(48)])
        gp = c.ps_reserve(4)
        for n in range(48):
            wt, wb = pf.next()
            kind = n // 8
            if kind in (2, 5):
                gi = 0 if kind == 2 else 1
                col = (n % 8) * 128
                pp, ppb = gp[gi * 2 + col // 512]
                for k in range(8):
                    c.mm(pp[0:NSEG, col % 512:col % 512 + 128], actT[:, k, :], wt[:, k, :], k == 0, k == 7, [actTb, wb], [ppb])
            else:
                p, pb = c.ps()
                for k in range(8):
                    c.mm(p[:, 0:NSEG], wt[:, k, :], actT[:, k, :], k == 0, k == 7, [actTb, wb], [pb])
                c.act(modT[:, n, :], p[:, 0:NSEG], AF.Identity, [pb, adabb], [modTb], bias=adab[:, n:n + 1])
        for gi in range(2):
            for hf in range(2):
                pp, ppb = gp[gi * 2 + hf]
                c.tt("dve", gtok[:, gi, hf * 512:(hf + 1) * 512], pp[0:NSEG, :], gbrow[:, gi, hf * 512:(hf + 1) * 512], ALU.add, [ppb, gbrowb], [gtokb])
        c.ps_release()
        gb_es.close()
        S.barrier()
        sc1, sc1b = lt("sc1", [128, 2, 8, NSEG])
        c.ts("dve", sc1[:, 0], modT[:, 8:16, :], 1.0, None, ALU.add, None, [modTb], [sc1b])
        c.ts("dve", sc1[:, 1], modT[:, 32:40, :], 1.0, None, ALU.add, None, [modTb], [sc1b])

        def transpose_mod(which):
            shoff = 0 if which == 0 else 24
            for j in range(NTILE):
                seg = j // 2
                for k4 in range(2):
                    p, pb = c.ps()
                    for kk in range(4):
                        k = k4 * 4 + kk
                        c.tr(p[:, kk * 128:(kk + 1) * 128], xt[:, j, k * 128:(k + 1) * 128], ident[:], [xb[j], identb], [pb])
                    for kk in range(4):
                        k = k4 * 4 + kk
                        c.act(hT[:, k, j * 128:(j + 1) * 128], p[:, kk * 128:(kk + 1) * 128], AF.Identity, [pb, modTb, sc1b], [hTb],
                              bias=modT[:, shoff + k, seg:seg + 1], scale=sc1[:, which, k, seg:seg + 1])

        def gbc_build(gi, seg, dst, dstb):
            for hf in range(2):
                p, pb = c.ps()
                c.mm(p[:, :], sel[:, seg, :], gtok[:, gi, hf * 512:(hf + 1) * 512], True, True, [selb, gtokb], [pb])
                c.cp("act", dst[:, hf * 512:(hf + 1) * 512], p[:, :], [pb], [dstb])

        def load_lnv(lnv, lnvb, which):
            for i, v in enumerate([ln1_g, ln1_b] if which == 0 else [ln2_g, ln2_b]):
                c.dma("sp", lnv[:, i, :], v[l:l + 1, :].partition_broadcast(128), W=[lnvb], sem="lnv")

        def layer_norm_tile(j, vt, vb, lnv, lnvb, stats, statsb, mv, mvb):
            for hf in range(2):
                S.op("dve", lambda e, hf=hf: e.bn_stats(out=stats[:, hf, :], in_=vt[:, hf * 512:(hf + 1) * 512]), reads=[vb], writes=[statsb])
            S.op("dve", lambda e: e.bn_aggr(out=mv[:, 0:2], in_=stats[:]), reads=[statsb], writes=[mvb])
            c.act(mv[:, 2:3], mv[:, 1:2], AF.Sqrt, [mvb, epsb], [mvb], bias=eps5[:, 0:1])
            S.op("dve", lambda e: e.reciprocal(out=mv[:, 3:4], in_=mv[:, 2:3]), reads=[mvb], writes=[mvb])
            c.ts("dve", vt[:], vt[:], mv[:, 0:1], mv[:, 3:4], ALU.subtract, ALU.mult, [vb, mvb], [vb])
            c.tt("pool", vt[:], vt[:], lnv[:, 0, :], ALU.mult, [vb, lnvb], [vb])
            c.tt("pool", xt[:, j, :], vt[:], lnv[:, 1, :], ALU.add, [vb, lnvb], [xb[j]])

        eps5, epsb = lt("eps5", [128, 2])
        c.memset("pool", eps5[:, 0:1], 1e-5, [epsb])
        c.memset("pool", eps5[:, 1:2], 1e-6, [epsb])

        dump("modT", modT[:], [128, 48, NSEG], modTb)
        dump("gtok", gtok[:], [NSEG, 2, D], gtokb)
        dump("bcol", bcol[:], [128, 80], bcolb)
        S.mark(f"L{l} transpose0")
        transpose_mod(0)
        dump("hT", hT[:], [128, 8, NT], hTb, bf=True)

        mx = contextlib.ExitStack()

        def mt(name, shape, dt=F32, mx=mx, l=l):
            t = mx.enter_context(nc.sbuf_tensor(f"L{l}m_{name}", shape, dt))
            return t, Buf(f"L{l}m_{name}")

        yaT = mx.enter_context(nc.sbuf_tensor(f"L{l}_yaT", [128, 8, NT], BF16))
        yaTb = [Buf(f"ya{h}") for h in range(8)]

        wl = w_in[l]

        def proj_fm(wt, wb, evac):
            for pi, (t0, n) in enumerate(PIECES):
                p, pb = c.ps()
                for k in range(8):
                    c.mm(p[:, 0:n], wt[:, k, :], hT[:, k, t0:t0 + n], k == 0, k == 7, [wb, hTb], [pb])
                evac(pi, t0, n, p, pb)

        def proj_tm(wt, wb, gi, h, evac):
            for j0 in range(0, NTILE, 4):
                p, pb = c.ps()
                js = list(range(j0, min(j0 + 4, NTILE)))
                for ji, j in enumerate(js):
                    o = p[:, ji * 128:(ji + 1) * 128]
                    for k in range(8):
                        c.mm(o, hT[:, k, j * 128:(j + 1) * 128], wt[:, k, :], k == 0, False, [wb, hTb], [pb])
                    c.mm(o, ones32[32 * gi:32 * gi + 1, 0:128], brow[32 * gi:32 * gi + 1, h * 128:(h + 1) * 128], False, True, [ones32b, browb], [pb])
                evac(j0, len(js), p, pb)

        hg_es = contextlib.ExitStack()

        def ht(name, shape, dt=F32, hg_es=hg_es, l=l):
            t = hg_es.enter_context(nc.sbuf_tensor(f"L{l}h_{name}", shape, dt))
            return t, Buf(f"L{l}h_{name}")

        q32, q32b = ht("q32", [128, NT])
        gate, gateb = ht("gate", [128, NT], BF16)
        vtok, vtokb = ht("vtok", [128, NTILE, 128], BF16)
        sg, sgb = ht("sg", [128, NT])
        lg, lgb = sg, sgb
        bb, bbb = sg, sgb
        kk_, kkb = ht("kk", [128, NT], BF16)
        tmpa, tmpab = ht("tmpa", [128, NT])
        tmpe, tmpeb = ht("tmpe", [128, NT])
        qtil = [ht(f"qtil{d}", [128, NT], BF16) for d in range(2)]
        ktil = [ht(f"ktil{d}", [128, NT], BF16) for d in range(2)]
        q64 = [ht(f"q64{d}", [128, NT], BF16) for d in range(2)]
        khT, khTb = kk_, kkb
        khat = [ht(f"khat{d}", [128, 2, NTILE, 128], BF16) for d in range(2)]
        for d in range(2):
            c.memset("pool", khat[d][0][:], 0.0, [khat[d][1]])
        ecol = [ht(f"ecol{d}", [128, 2, 20]) for d in range(2)]
        Sst = [ht(f"Sst{d}", [128, 128]) for d in range(2)]
        Sst2 = [ht(f"Sst2{d}", [128, 128]) for d in range(2)]
        Sbf = [ht(f"Sbf{d}", [128, 20, 128], BF16) for d in range(2)]
        stage = [ht(f"stage{i}", [128, 128]) for i in range(2)]
        attsb2 = [ht(f"attsb{i}", [128, 4, 128], BF16) for i in range(2)]
        sq, sqb = ht("sq", [128, 512], BF16)
        rstd, rstdb = ht("rstd", [128, 512])
        t1, t1b = ht("t1", [128, 512])
        stg_i = [0]

        print("SBUF after hgrn alloc:", nc.sbuf_bytes_remaining)
        hcols = [0, 3072, 4096, 1024, 2048]
        fns = []
        for h in range(8):
            for g in range(5):
                fns.append(w8fn(wl, hcols[g] + h * 128))
        pf = Prefetch(r8, fns)

        for h in range(8):
            S.mark(f"L{l} hgrn h{h} proj+gates")
            wt, wb = pf.next()

            def ev_q(pi, t0, n, p, pb, h=h):
                c.act(q32[:, t0:t0 + n], p[:, 0:n], AF.Identity, [pb, bcolb], [q32b], bias=bcol[:, h:h + 1])
            proj_fm(wt, wb, ev_q)
            wt, wb = pf.next()

            def ev_v(j0, nj, p, pb):
                c.cp("act", vtok[:, j0:j0 + nj, :].rearrange("p j f -> p (j f)"), p[:, 0:nj * 128], [pb], [vtokb])
            proj_tm(wt, wb, 0, h, ev_v)
            wt, wb = pf.next()

            def ev_g(pi, t0, n, p, pb, h=h):
                c.act(gate[:, t0:t0 + n], p[:, 0:n], AF.Silu, [pb, bcolb], [gateb], bias=bcol[:, 32 + h:33 + h])
            proj_fm(wt, wb, ev_g)

            for d in range(2):
                wt, wb = pf.next()

                def ev_z(pi, t0, n, p, pb, h=h, d=d):
                    c.act(sg[:, t0:t0 + n], p[:, 0:n], AF.Sigmoid, [pb, bcolb], [sgb], bias=bcol[:, 8 + 8 * d + h:9 + 8 * d + h])
                proj_fm(wt, wb, ev_z)
                c.ts("dve", sg[:], sg[:], omlt[:, d, l, h:h + 1], lbt[:, d, l, h:h + 1], ALU.mult, ALU.add, [sgb, omltb, lbtb], [sgb])
                c.ts("pool", kk_[:], sg[:], -1.0, 1.0, ALU.mult, ALU.add, [sgb], [kkb])
                c.act(sg[:], sg[:], AF.Ln, [sgb], [sgb])
                if d == 0:
                    S.op("dve", lambda e: e.tensor_tensor_scan(out=sg[:], data0=smask[:], data1=sg[:], initial=0.0, op0=ALU.mult, op1=ALU.add),
                         reads=[smaskb, sgb], writes=[sgb])
                else:
                    S.op("dve", lambda e: e.tensor_tensor_scan(out=sg[:, ::-1], data0=smask[:], data1=sg[:, ::-1], initial=0.0, op0=ALU.mult, op1=ALU.add),
                         reads=[smaskb, sgb], writes=[sgb])
                if h == 0:
                    dump(f"b{d}", sg[:], [128, NT], sgb)
                bv = bb[:].rearrange("p (c j) -> p c j", j=64)
                last = 63 if d == 0 else 0
                ec, ecb = ecol[d]
                S.op("act", lambda e, ec=ec, bv=bv, last=last: e.activation(out=ec[:, 1, :], in_=bv[:, :, last], func=AF.Exp), reads=[bbb], writes=[ecb])
                c.act(tmpe[:], sg[:], AF.Exp, [sgb], [tmpeb])
                c.tt("pool", q64[d][0][:], q32[:], tmpe[:], ALU.mult, [q32b, tmpeb], [q64[d][1]])
                b4 = bb[:].rearrange("p (c h j) -> p c h j", h=2, j=32)
                t4 = tmpa[:].rearrange("p (c h j) -> p c h j", h=2, j=32)
                if d == 0:
                    c.cp("pool", t4[:, :, 0, :], b4[:, :, 0, :], [bbb], [tmpab])
                    c.tt("dve", t4[:, :, 1, :], b4[:, :, 1, :], b4[:, :, 0, 31:32].to_broadcast([128, 20, 32]), ALU.subtract, [bbb], [tmpab])
                else:
                    c.cp("pool", t4[:, :, 1, :], b4[:, :, 1, :], [bbb], [tmpab])
                    c.tt("dve", t4[:, :, 0, :], b4[:, :, 0, :], b4[:, :, 1, 0:1].to_broadcast([128, 20, 32]), ALU.subtract, [bbb], [tmpab])
                c.act(tmpe[:], tmpa[:], AF.Exp, [tmpab], [tmpeb])
                qt, qtb = qtil[d]
                c.tt("pool", qt[:], q32[:], tmpe[:], ALU.mult, [q32b, tmpeb], [qtb])
                c.act(tmpe[:], tmpa[:], AF.Exp, [tmpab], [tmpeb], scale=-1.0)
                kt, ktb = ktil[d]
                c.tt("pool", kt[:], kk_[:], tmpe[:], ALU.mult, [kkb, tmpeb], [ktb])
                tv = tmpa[:].rearrange("p (c j) -> p c j", j=64)
                c.tt("dve", tv, bv, bv[:, :, last:last + 1].to_broadcast([128, 20, 64]), ALU.subtract, [bbb], [tmpab])
                c.act(tmpe[:], tmpa[:], AF.Exp, [tmpab], [tmpeb], scale=-1.0)
                c.tt("pool", khT[:], kk_[:], tmpe[:], ALU.mult, [kkb, tmpeb], [khTb])
                kh, khb = khat[d]
                for j0 in (0, 4, 8):
                    p, pb = c.ps()
                    pv = p[:].bitcast(BF16)
                    nj = min(4, NTILE - j0)
                    for ji in range(nj):
                        j = j0 + ji
                        c.tr(pv[:, ji * 128:(ji + 1) * 128], khT[:, j * 128:(j + 1) * 128], identh[:], [khTb, identhb], [pb])
                    c.cp("act", kh[0:64, 0, j0:j0 + nj, :].rearrange("p j f -> p (j f)"), pv[0:64, 0:nj * 128], [pb], [khb])
                    c.cp("act", kh[64:128, 1, j0:j0 + nj, :].rearrange("p j f -> p (j f)"), pv[64:128, 0:nj * 128], [pb], [khb])

            if h == 0:
                dump("q32", q32[:], [128, NT], q32b)
                dump("gate", gate[:], [128, NT], gateb, bf=True)
                dump("vtok", vtok[:], [128, NTILE, 128], vtokb, bf=True)
                for d in range(2):
                    dump(f"qtil{d}", qtil[d][0][:], [128, NT], qtil[d][1], bf=True)
                    dump(f"ktil{d}", ktil[d][0][:], [128, NT], ktil[d][1], bf=True)
                    dump(f"khat{d}", khat[d][0][:], [128, 2, NTILE, 128], khat[d][1], bf=True)
                    dump(f"q64{d}", q64[d][0][:], [128, NT], q64[d][1], bf=True)
            S.mark(f"L{l} hgrn h{h} chain")
            for d in range(2):
                sts = [Sst[d], Sst2[d]]
                cur = 0
                sbf, sbfb = Sbf[d]
                ec, ecb = ecol[d]
                kh, khb = khat[d]
                order = list(range(20)) if d == 0 else list(range(19, -1, -1))
                for g in range(5):
                    chunks = order[g * 4:(g + 1) * 4]
                    p, pb = c.ps()
                    for i, cc in enumerate(chunks):
                        j = cc // 2
                        r0 = (cc % 2) * 64
                        c.mm(p[:, i * 128:(i + 1) * 128], kh[:, cc % 2, j, :], vtok[:, j, :], True, True, [khb, vtokb], [pb])
                    for i, cc in enumerate(chunks):
                        seg = cc // 4
                        entry = (i == 0)
                        final = (i == 3)
                        st_, stb = sts[cur]
                        if entry:
                            if seg == 4:
                                c.memset("dve", st_[:], 0.0, [stb])
                            elif (d == 0 and seg == 0) or (d == 1 and seg == 3):
                                c.dma("sp", st_[:], hinit[l, d, h], W=[stb], sem=f"hin{d}")
                            else:
                                c.ts("dve", st_[:], st_[:], flag[:, 0:1], None, ALU.mult, None, [stb, flagb], [stb])
                        c.cp("act", sbf[:, cc, :], st_[:], [stb], [sbfb])
                        nx, nxb = sts[1 - cur]
                        c.stt("dve", nx[:], st_[:], ec[:, 1, cc:cc + 1], p[:, i * 128:(i + 1) * 128], ALU.mult, ALU.add, [stb, ecb, pb], [nxb])
                        cur = 1 - cur
                        if final:
                            sgt, sgtb = stage[stg_i[0] % 2]
                            c.cp("pool", sgt[:], nx[:], [nxb], [sgtb])
                            c.dma("sp", o_h[seg, l, d, h], sgt[:], R=[sgtb], sem=f"ostage{stg_i[0] % 2}", is_out=True)
                            stg_i[0] += 1

            if h == 0:
                for d in range(2):
                    dump(f"Sbf{d}", Sbf[d][0][:], [128, 20, 128], Sbf[d][1], bf=True)
            S.mark(f"L{l} hgrn h{h} out")
            def emit_att(j):
                pa, pab = c.ps()
                tsl = slice(j * 128, (j + 1) * 128)
                for d in range(2):
                    c.mm(pa[:, d * 128:(d + 1) * 128], ktil[d][0][:, tsl], qtil[d][0][:, tsl], True, True, [ktil[d][1], qtil[d][1]], [pab])
                    c.mm(pa[:, 256 + d * 128:256 + (d + 1) * 128], ktil[d][0][:, tsl], q64[d][0][:, tsl], True, True, [ktil[d][1], q64[d][1]], [pab])
                a_, a_b = attsb2[j % 2]
                c.tt("dve", a_[:].rearrange("p d t -> p (d t)"), pa[:, 0:512], amask[:].rearrange("p d t -> p (d t)"), ALU.mult, [pab, amaskb], [a_b])

            emit_att(0)
            for pi, (t0, n) in enumerate(PIECES):
                po, pob = c.ps()
                for ji in range(n // 128):
                    j = t0 // 128 + ji
                    if j + 1 < NTILE:
                        emit_att(j + 1)
                    a_, a_b = attsb2[j % 2]
                    o = po[:, ji * 128:(ji + 1) * 128]
                    for i4 in range(4):
                        c.mm(o, vtok[:, j, :], a_[:, i4, :], i4 == 0, False, [vtokb, a_b], [pob])
                    for d in range(2):
                        for cc in (2 * j, 2 * j + 1):
                            oc = po[:, ji * 128 + (cc % 2) * 64: ji * 128 + (cc % 2) * 64 + 64]
                            c.mm(oc, Sbf[d][0][:, cc, :], q64[d][0][:, cc * 64:(cc + 1) * 64], False, (d == 1 and cc == 2 * j + 1),
                                 [Sbf[d][1], q64[d][1]], [pob])
                c.act(sq[:, 0:n], po[:, 0:n], AF.Square, [pob], [sqb])
                pss, pssb = c.ps()
                c.mm(pss[:, 0:n], onesbf[:], sq[:, 0:n], True, True, [onesbfb, sqb], [pssb])
                c.act(rstd[:, 0:n], pss[:, 0:n], AF.Sqrt, [pssb, epsb], [rstdb], bias=eps5[:, 1:2], scale=1.0 / 128)
                S.op("dve", lambda e, n=n: e.reciprocal(out=rstd[:, 0:n], in_=rstd[:, 0:n]), reads=[rstdb], writes=[rstdb])
                c.stt("dve", t1[:, 0:n], po[:, 0:n], hg[:, 0:1], rstd[:, 0:n], ALU.mult, ALU.mult, [pob, hgb, rstdb], [t1b])
                c.tt("pool", yaT[:, h, t0:t0 + n], t1[:, 0:n], gate[:, t0:t0 + n], ALU.mult, [t1b, gateb], [yaTb[h]])
        dump("yaT", yaT[:], [128, 8, NT], yaTb, bf=True)
        hg_es.close()
        S.barrier()
        ybT = mx.enter_context(nc.sbuf_tensor(f"L{l}_ybT", [128, 8, NT], BF16))
        ybTb = [Buf(f"yb{h}") for h in range(8)]

        ml_es = contextlib.ExitStack()

        def mlt(name, shape, dt=F32, ml_es=ml_es, l=l):
            t = ml_es.enter_context(nc.sbuf_tensor(f"L{l}b_{name}", shape, dt))
            return t, Buf(f"L{l}b_{name}")

        S.mark(f"L{l} mlstm gates")
        gw, gwb = mlt("gw", [128, 8, 32], BF16)
        c.dma("pool", gw[:], wl[:, 8192:8224].rearrange("(k p) n -> p k n", p=128), W=[gwb], sem="gw")
        utok, utokb = mlt("utok", [128, NTILE, 32])
        abc, abcb = mlt("abc", [128, 2, 8, NTILE])
        mfin, mfinb = mlt("mfin", [8, 2, NSEG])
        ncols, ncolsb = mlt("ncols", [128, NSEG, 2, 8])
        c.memset("pool", ncols[:], 0.0, [ncolsb])
        for d in range(2):
            with contextlib.ExitStack() as ge:
                def gt(name, shape, dt=F32, ge=ge):
                    t = ge.enter_context(nc.sbuf_tensor(f"L{l}g_{name}", shape, dt))
                    return t, Buf(f"L{l}g_{name}")
                ig, igb = gt(f"ig{d}", [8, NT])
                lp, lpb = gt(f"lp{d}", [8, NT])
                P_, Pb = gt(f"P{d}", [8, NT])
                G_, Gb = ig, igb
                nb, nbb = gt(f"nb{d}", [8, 1])
                gmax, gmaxb = gt(f"gmax{d}", [8, NTILE])
                Mt, Mtb = gt(f"M{d}", [8, NTILE])
                ms, msb = gt(f"ms{d}", [8, NTILE])
                A_, Ab = gt(f"A{d}", [8, NTILE])
                mcur, mcurb = gt(f"mcur{d}", [8, 1])
                arhs, arhsb = gt(f"arhs{d}", [8, 8, NTILE])
                c.tt("dve", nb[:], gbi[:, 2 * d + 1:2 * d + 2], fbt[:, d:d + 1], ALU.add, [gbib, fbtb], [nbb])
                c.ts("dve", nb[:], nb[:], -1.0, None, ALU.mult, None, [nbb], [nbb])
                for pi, (t0, n) in enumerate(PIECES):
                    p, pb = c.ps()
                    for k in range(8):
                        c.mm(p[0:8, 0:n], gw[:, k, 16 * d:16 * d + 8], hT[:, k, t0:t0 + n], k == 0, k == 7, [gwb, hTb], [pb])
                    c.act(ig[:, t0:t0 + n], p[0:8, 0:n], AF.Identity, [pb, gbib], [igb], bias=gbi[:, 2 * d:2 * d + 1])
                    p, pb = c.ps()
                    for k in range(8):
                        c.mm(p[0:8, 0:n], gw[:, k, 16 * d + 8:16 * d + 16], hT[:, k, t0:t0 + n], k == 0, k == 7, [gwb, hTb], [pb])
                    c.act(lp[:, t0:t0 + n], p[0:8, 0:n], AF.Exp, [pb, nbb], [lpb], bias=nb[:, 0:1], scale=-1.0)
                c.act(lp[:], lp[:], AF.Ln, [lpb], [lpb], bias=1.0)
                for cc in range(NTILE):
                    if d == 0:
                        S.op("dve", lambda e, P_=P_, lp=lp, cc=cc: e.tensor_tensor_scan(out=P_[:, cc * 128:(cc + 1) * 128], data0=onesf8[:, :], data1=lp[:, cc * 128:(cc + 1) * 128], initial=0.0, op0=ALU.mult, op1=ALU.add),
                             reads=[onesf8b, lpb], writes=[Pb])
                    else:
                        S.op("dve", lambda e, P_=P_, lp=lp, cc=cc: e.tensor_tensor_scan(out=P_[:, cc * 128:(cc + 1) * 128][:, ::-1], data0=onesf8[:, :], data1=lp[:, cc * 128:(cc + 1) * 128][:, ::-1], initial=0.0, op0=ALU.mult, op1=ALU.add),
                             reads=[onesf8b, lpb], writes=[Pb])
                c.tt("dve", G_[:], ig[:], P_[:], ALU.add, [igb, Pb], [Gb])
                S.op("dve", lambda e, gmax=gmax, G_=G_: e.tensor_reduce(out=gmax[:], in_=G_[:].rearrange("p (c j) -> p c j", j=128), axis=AX.X, op=ALU.max),
                     reads=[Gb], writes=[gmaxb])
                Pv = P_[:].rearrange("p (c j) -> p c j", j=128)
                lastj = 127 if d == 0 else 0
                order = list(range(NTILE)) if d == 0 else list(range(NTILE - 1, -1, -1))
                for cc in order:
                    seg = cc // 2
                    entry = (cc % 2 == 0) if d == 0 else (cc % 2 == 1)
                    final = (cc % 2 == 1) if d == 0 else (cc % 2 == 0)
                    if entry:
                        if seg == 4:
                            c.memset("dve", mcur[:], 0.0, [mcurb])
                        elif (d == 0 and seg == 0) or (d == 1 and seg == 3):
                            with nc.allow_non_contiguous_dma(reason="tiny"):
                                c.dma("sp", mcur[:], minit[l, d].rearrange("(h o) -> h o", o=1), W=[mcurb], sem=f"min{d}")
                        else:
                            c.ts("dve", mcur[:], mcur[:], flag[0:8, 0:1], None, ALU.mult, None, [mcurb, flagb], [mcurb])
                    c.cp("dve", ms[:, cc:cc + 1], mcur[:], [mcurb], [msb])
                    c.tt("dve", Mt[:, cc:cc + 1], mcur[:], gmax[:, cc:cc + 1], ALU.max, [mcurb, gmaxb], [Mtb])
                    c.tt("dve", mcur[:], Mt[:, cc:cc + 1], Pv[:, cc, lastj:lastj + 1], ALU.subtract, [Mtb, Pb], [mcurb])
                    if final:
                        c.cp("dve", mfin[:, d, seg:seg + 1], mcur[:], [mcurb], [mfinb])
                c.tt("dve", A_[:], ms[:], Mt[:], ALU.subtract, [msb, Mtb], [Ab])
                c.act(A_[:], A_[:], AF.Exp, [Ab], [Ab])
                Gv = G_[:].rearrange("p (c j) -> p c j", j=128)
                Mb_ = Mt[:].rearrange("p (c o) -> p c o", o=1).to_broadcast([8, NTILE, 128])
                c.tt("dve", Gv, Gv, Mb_, ALU.subtract, [Gb, Mtb], [Gb])
                c.tt("dve", Pv, Pv, Mb_, ALU.subtract, [Pb, Mtb], [Pb])
                c.act(G_[:], G_[:], AF.Exp, [Gb], [Gb])
                c.act(P_[:], P_[:], AF.Exp, [Pb], [Pb])
                for j in range(NTILE):
                    p, pb = c.ps()
                    c.tr(p[:, 0:8], G_[:, j * 128:(j + 1) * 128], ident[0:8, 0:8], [Gb, identb], [pb])
                    c.tr(p[:, 8:16], P_[:, j * 128:(j + 1) * 128], ident[0:8, 0:8], [Pb, identb], [pb])
                    c.cp("dve", utok[:, j, d * 16:(d + 1) * 16], p[:, 0:16], [pb], [utokb])
                c.tt("dve", arhs[:], A_[:].rearrange("p (o c) -> p o c", o=1).to_broadcast([8, 8, NTILE]),
                     ident[0:8, 0:8].rearrange("p (h o) -> p h o", o=1).to_broadcast([8, 8, NTILE]), ALU.mult, [Ab, identb], [arhsb])
                p, pb = c.ps()
                c.mm(p[:, 0:80], onesf8[:, :], arhs[:].rearrange("p h c -> p (h c)"), True, True, [arhsb, onesf8b], [pb])
                c.cp("dve", abc[:, d].rearrange("p h c -> p (h c)"), p[:, 0:80], [pb], [abcb])
            S.barrier()
        with nc.allow_non_contiguous_dma(reason="tiny m out"):
            for seg in range(NSEG):
                for d in range(2):
                    c.dma("sp", o_m[seg, l, d].rearrange("(h o) -> h o", o=1), mfin[:, d, seg:seg + 1], R=[mfinb], sem="om", is_out=True)
        for d in range(2):
            c.ts("dve", utok[:, :, d * 16:d * 16 + 8], utok[:, :, d * 16:d * 16 + 8], 128.0 ** -0.5, None, ALU.mult, None, [utokb], [utokb])

        dump("utok", utok[:], [128, NTILE, 32], utokb)
        dump("abc", abc[:], [128, 2, 8, NTILE], abcb)
        dump("mfin", mfin[:], [8, 2, NSEG], mfinb)
        xcA, xcAb = mlt("xcA", [128, 3, 1154], BF16)
        xcB, xcBb = mlt("xcB", [128, 1, 258], BF16)
        c.memset("pool", xcA[:], 0.0, [xcAb])
        c.memset("pool", xcB[:], 0.0, [xcBb])
        dgw, dgwb = mlt("dgw", [128, 9, 128], BF16)
        dgf, dgfb = mlt("dgf", [128, 9, 128], BF16)
        wcol, wcolb = mlt("wcol", [128, 9])
        caT, caTb = mlt("caT", [128, NT], BF16)
        qbT, qbTb = mlt("qbT", [128, NT], BF16)
        kbT, kbTb = mlt("kbT", [128, NT], BF16)
        kp, kpb = mlt("kp", [128, 2, NTILE, 128], BF16)
        vaug, vaugb = mlt("vaug", [128, NTILE, 129], BF16)
        c.memset("pool", vaug[:], 1.0, [vaugb])
        og, ogb = mlt("og", [128, NTILE, 128], BF16)
        Chat = [mlt(f"Chat{d}", [128, 129]) for d in range(2)]
        Chat2 = [mlt(f"Chat2{d}", [128, 129]) for d in range(2)]
        Cbf = [mlt(f"Cbf{d}", [128, NTILE, 129], BF16) for d in range(2)]
        ssb, ssbb = mlt("ssb", [128, 128], BF16)
        hacc, haccb = mlt("hacc", [128, 128])
        dn, dnb = mlt("dn", [128, 4])
        bst, bstb = mlt("bst", [128, 6])
        bmv, bmvb = mlt("bmv", [128, 4])
        ybk, ybkb = mlt("ybk", [128, 128], BF16)
        y2, y2b = mlt("y2", [128, 128])
        cstage = [mlt(f"cstage{i}", [128, 128]) for i in range(2)]
        cst_i = [0]

        fns = []
        for h in range(8):
            for g in range(3):
                fns.append(w8fn(wl, 5120 + g * 1024 + h * 128))
        pf = Prefetch(r8, fns)

        for h in range(8):
            S.mark(f"L{l} mlstm h{h} proj+conv")
            wt, wb = pf.next()

            def ev_qk(pi, t0, n, p, pb, h=h):
                if pi < 2:
                    c.act(xcA[:, 0, PADA + t0:PADA + t0 + n], p[:, 0:n], AF.Identity, [pb, bcolb], [xcAb], bias=bcol[:, 40 + h:41 + h])
                else:
                    c.act(xcB[:, 0, 1:257], p[:, 0:n], AF.Identity, [pb, bcolb], [xcBb], bias=bcol[:, 40 + h:41 + h])
            proj_fm(wt, wb, ev_qk)
            for m_ in range(2):
                c.tt("pool", xcA[:, 1 + m_, :], xcA[:, 0, :], cmask[:, m_, :], ALU.mult, [xcAb, cmaskb], [xcAb])
            c.cp("dve", wcol[:], cvw[:, :, h], [cvwb], [wcolb])
            for tap in range(9):
                c.ts("dve", dgw[:, tap, :], identh[:], wcol[:, tap:tap + 1], None, ALU.mult, None, [identhb, wcolb], [dgwb])
            c.ts("dve", wcol[:, 0:3], wcol[:, 0:3], flag[:, 0:1], None, ALU.mult, None, [wcolb, flagb], [wcolb])
            c.ts("dve", wcol[:, 6:9], wcol[:, 6:9], flag[:, 0:1], None, ALU.mult, None, [wcolb, flagb], [wcolb])
            for tap in (0, 1, 2, 6, 7, 8):
                c.ts("dve", dgf[:, tap, :], identh[:], wcol[:, tap:tap + 1], None, ALU.mult, None, [identhb, wcolb], [dgfb])
            for pi, (t0, n) in enumerate(PIECES):
                p, pb = c.ps()
                if pi < 2:
                    taps = [(kh_, kw_) for kh_ in range(3) for kw_ in range(3)]
                    for ti, (kh_, kw_) in enumerate(taps):
                        off = (kh_ - 1) * 64 + (kw_ - 1)
                        srcidx = {0: 1, 1: 0, 2: 2}[kw_]
                        wsrc = dgw if kh_ == 1 else dgf
                        wsrcb = dgwb if kh_ == 1 else dgfb
                        c.mm(p[:, 0:n], wsrc[:, kh_ * 3 + kw_, :], xcA[:, srcidx, PADA + t0 + off:PADA + t0 + off + n], ti == 0, ti == 8, [wsrcb, xcAb], [pb])
                else:
                    for kw_ in range(3):
                        c.mm(p[:, 0:n], dgw[:, 3 + kw_, :], xcB[:, 0, kw_:kw_ + 256], kw_ == 0, kw_ == 2, [dgwb, xcBb], [pb])
                c.act(caT[:, t0:t0 + n], p[:, 0:n], AF.Silu, [pb, cvbb], [caTb], bias=cvb[:, h:h + 1])
            wq, wqb = rsm.load(lambda t, h=h: [(t[:], w_mq[l, h])])
            wk, wkb = rsm.load(lambda t, h=h: [(t[:], w_mk[l, h])])
            for pi, (t0, n) in enumerate(PIECES):
                p, pb = c.ps()
                c.mm(p[:, 0:n], wq[:], caT[:, t0:t0 + n], True, True, [wqb, caTb], [pb])
                c.cp("act", qbT[:, t0:t0 + n], p[:, 0:n], [pb], [qbTb])
                p, pb = c.ps()
                c.mm(p[:, 0:n], wk[:], caT[:, t0:t0 + n], True, True, [wkb, caTb], [pb])
                c.cp("dve", kbT[:, t0:t0 + n], p[:, 0:n], [pb], [kbTb])
            for j0 in (0, 4, 8):
                p, pb = c.ps()
                nj = min(4, NTILE - j0)
                for ji in range(nj):
                    j = j0 + ji
                    c.mm(p[:, ji * 128:(ji + 1) * 128], caT[:, j * 128:(j + 1) * 128], wk[:], True, True, [caTb, wkb], [pb])
                for ji in range(nj):
                    j = j0 + ji
                    for d in range(2):
                        c.act(kp[:, d, j, :], p[:, ji * 128:(ji + 1) * 128], AF.Copy, [pb, utokb], [kpb], scale=utok[:, j, d * 16 + h:d * 16 + h + 1])
            wt, wb = pf.next()

            def ev_bv(j0, nj, p, pb):
                c.cp("act", vaug[:, j0:j0 + nj, 0:128], p[:, 0:nj * 128].rearrange("p (j f) -> p j f", f=128), [pb], [vaugb])
            proj_tm(wt, wb, 1, h, ev_bv)
            wt, wb = pf.next()

            def ev_bo(j0, nj, p, pb):
                c.act(og[:, j0:j0 + nj, :].rearrange("p j f -> p (j f)"), p[:, 0:nj * 128], AF.Sigmoid, [pb], [ogb])
            proj_tm(wt, wb, 2, h, ev_bo)

            if h == 0:
                dump("caT", caT[:], [128, NT], caTb, bf=True)
                dump("qbT", qbT[:], [128, NT], qbTb, bf=True)
                dump("kbT", kbT[:], [128, NT], kbTb, bf=True)
                dump("kp", kp[:], [128, 2, NTILE, 128], kpb, bf=True)
                dump("vaug", vaug[:], [128, NTILE, 129], vaugb, bf=True)
                dump("og", og[:], [128, NTILE, 128], ogb, bf=True)
            S.mark(f"L{l} mlstm h{h} chain")
            for d in range(2):
                chs = [Chat[d], Chat2[d]]
                cur = 0
                cb, cbb = Cbf[d]
                order = list(range(NTILE)) if d == 0 else list(range(NTILE - 1, -1, -1))
                for g in range(NSEG):
                    tiles = order[g * 2:(g + 1) * 2]
                    p, pb = c.ps()
                    for i, j in enumerate(tiles):
                        c.mm(p[:, i * 129:i * 129 + 129], kp[:, d, j, :], vaug[:, j, :], True, True, [kpb, vaugb], [pb])
                    for i, j in enumerate(tiles):
                        seg = j // 2
                        entry = (i == 0)
                        final = (i == 1)
                        ch, chb = chs[cur]
                        if entry:
                            if seg == 4:
                                c.memset("dve", ch[:], 0.0, [chb])
                            elif (d == 0 and seg == 0) or (d == 1 and seg == 3):
                                c.dma("sp", ch[:, 0:128], cinit[l, d, h], W=[chb], sem=f"cin{d}")
                                c.dma("sp", ch[:, 128:129], ninit[l, d, h].rearrange("(k o) -> k o", o=1), W=[chb], sem=f"cin{d}")
                            else:
                                c.ts("dve", ch[:], ch[:], flag[:, 0:1], None, ALU.mult, None, [chb, flagb], [chb])
                        acol = abc[:, d, h, j:j + 1]
                        c.act(cb[:, j, :], ch[:], AF.Copy, [chb, abcb], [cbb], scale=acol)
                        nx, nxb = chs[1 - cur]
                        c.stt("dve", nx[:], ch[:], acol, p[:, i * 129:i * 129 + 129], ALU.mult, ALU.add, [chb, abcb, pb], [nxb])
                        cur = 1 - cur
                        if final:
                            sgt, sgtb = cstage[cst_i[0] % 2]
                            c.cp("pool", sgt[:], nx[:, 0:128], [nxb], [sgtb])
                            c.dma("sp", o_C[seg, l, d, h], sgt[:], R=[sgtb], sem=f"cstage{cst_i[0] % 2}", is_out=True)
                            cst_i[0] += 1
                            c.cp("pool", ncols[:, seg, d, h:h + 1], nx[:, 128:129], [nxb], [ncolsb])

            S.mark(f"L{l} mlstm h{h} out")
            for j in range(NTILE):
                for d in range(2):
                    psT, psTb = c.ps()
                    c.mm(psT[:, 0:128], kbT[:, j * 128:(j + 1) * 128], qbT[:, j * 128:(j + 1) * 128], True, True, [kbTb, qbTb], [psTb])
                    c.stt("dve", ssb[:], psT[:, 0:128], utok[:, j, d * 16 + h:d * 16 + h + 1], bmask[:, d, :], ALU.mult, ALU.mult, [psTb, utokb, bmaskb], [ssbb])
                    pn, pnb = c.ps()
                    c.mm(pn[:, 0:129], ssb[:], vaug[:, j, :], True, False, [ssbb, vaugb], [pnb])
                    c.mm(pn[:, 0:129], qbT[:, j * 128:(j + 1) * 128], Cbf[d][0][:, j, :], False, True, [qbTb, Cbf[d][1]], [pnb])
                    c.cp("dve", dn[:, 3:4], pn[:, 128:129], [pnb], [dnb])
                    c.stt("dve", dn[:, 0:1], dn[:, 3:4], -1.0, dn[:, 3:4], ALU.mult, ALU.max, [dnb], [dnb])
                    c.tt("dve", dn[:, 1:2], dn[:, 0:1], utok[:, j, d * 16 + 8 + h:d * 16 + 8 + h + 1], ALU.max, [dnb, utokb], [dnb])
                    S.op("dve", lambda e: e.reciprocal(out=dn[:, 2:3], in_=dn[:, 1:2]), reads=[dnb], writes=[dnb])
                    if d == 0:
                        c.act(hacc[:], pn[:, 0:128], AF.Copy, [pnb, dnb], [haccb], scale=dn[:, 2:3])
                    else:
                        c.stt("dve", hacc[:], pn[:, 0:128], dn[:, 2:3], hacc[:], ALU.mult, ALU.add, [pnb, dnb, haccb], [haccb])
                S.op("dve", lambda e: e.bn_stats(out=bst[:], in_=hacc[:]), reads=[haccb], writes=[bstb])
                S.op("dve", lambda e: e.bn_aggr(out=bmv[:, 0:2], in_=bst[:]), reads=[bstb], writes=[bmvb])
                c.act(bmv[:, 2:3], bmv[:, 1:2], AF.Sqrt, [bmvb, epsb], [bmvb], bias=eps5[:, 1:2])
                S.op("dve", lambda e: e.reciprocal(out=bmv[:, 3:4], in_=bmv[:, 2:3]), reads=[bmvb], writes=[bmvb])
                c.ts("dve", y2[:], hacc[:], bmv[:, 0:1], bmv[:, 3:4], ALU.subtract, ALU.mult, [haccb, bmvb], [y2b])
                c.tt("dve", y2[:], y2[:], mgbc[:], ALU.mult, [y2b, mgbcb], [y2b])
                c.tt("dve", ybk[:], y2[:], og[:, j, :], ALU.mult, [y2b, ogb], [ybkb])
                pt, ptb = c.ps()
                ptv = pt[:].bitcast(BF16)
                c.tr(ptv[:, 0:128], ybk[:], identh[:], [ybkb, identhb], [ptb])
                c.cp("act", ybT[:, h, j * 128:(j + 1) * 128], ptv[:, 0:128], [ptb], [ybTb[h]])

        dump("ybT", ybT[:], [128, 8, NT], ybTb, bf=True)
        nrow, nrowb = mlt("nrow", [80, 128])
        p, pb = c.ps()
        c.tr(p[0:80, 0:128], ncols[:].rearrange("p s d h -> p (s d h)"), ident[:], [ncolsb, identb], [pb])
        c.cp("dve", nrow[:], p[0:80, 0:128], [pb], [nrowb])
        for seg in range(NSEG):
            c.dma("sp", o_n[seg, l].rearrange("d h k -> (d h) k"), nrow[seg * 16:(seg + 1) * 16, :], R=[nrowb], sem="on", is_out=True)
        ml_es.close()
        S.barrier()

        S.mark(f"L{l} merge")
        mg_es = contextlib.ExitStack()

        def mgt(name, shape, dt=F32, mg_es=mg_es, l=l):
            t = mg_es.enter_context(nc.sbuf_tensor(f"L{l}c_{name}", shape, dt))
            return t, Buf(f"L{l}c_{name}")
        mrg = mg_es.enter_context(nc.sbuf_tensor(f"L{l}_mrg", [128, 8, NT], BF16))
        mrgb = [Buf(f"mrg{e}") for e in range(8)]
        mg1 = contextlib.ExitStack()

        def mg1t(name, shape, dt=F32, mg1=mg1, l=l):
            t = mg1.enter_context(nc.sbuf_tensor(f"L{l}d_{name}", shape, dt))
            return t, Buf(f"L{l}d_{name}")
        sga, sgab = mg1t("sga", [128, 512])
        sgb_, sgbb = mg1t("sgb", [128, 512])
        m1, m1b = mg1t("m1", [128, 512])
        m2, m2b = mg1t("m2", [128, 512])
        fns = []
        for e_ in range(8):
            fns += [w8fn(wl, 8224 + e_ * 128), w8fn(wl, 9248 + e_ * 128), w8fn(w_ba[l], e_ * 128), w8fn(w_bb[l], e_ * 128)]
        pf = Prefetch(r8, fns, depth=4)
        for e_ in range(8):
            wga, wgab = pf.next()
            wgb, wgbb = pf.next()
            wa, wab = pf.next()
            wb_, wbb = pf.next()
            for pi, (t0, n) in enumerate(PIECES):
                p, pb = c.ps()
                for k in range(8):
                    c.mm(p[:, 0:n], wga[:, k, :], hT[:, k, t0:t0 + n], k == 0, k == 7, [wgab, hTb], [pb])
                c.act(sga[:, 0:n], p[:, 0:n], AF.Sigmoid, [pb, bcolb], [sgab], bias=bcol[:, 64 + e_:65 + e_])
                p, pb = c.ps()
                for k in range(8):
                    c.mm(p[:, 0:n], wgb[:, k, :], hT[:, k, t0:t0 + n], k == 0, k == 7, [wgbb, hTb], [pb])
                c.act(sgb_[:, 0:n], p[:, 0:n], AF.Sigmoid, [pb, bcolb], [sgbb], bias=bcol[:, 72 + e_:73 + e_])
                p, pb = c.ps()
                for k in range(8):
                    c.mm(p[:, 0:n], wa[:, k, :], yaT[:, k, t0:t0 + n], k == 0, k == 7, [wab, yaTb[k]], [pb])
                c.tt("dve", m1[:, 0:n], p[:, 0:n], sga[:, 0:n], ALU.mult, [pb, sgab], [m1b])
                p, pb = c.ps()
                for k in range(8):
                    c.mm(p[:, 0:n], wb_[:, k, :], ybT[:, k, t0:t0 + n], k == 0, k == 7, [wbb, ybTb[k]], [pb])
                c.tt("dve", m2[:, 0:n], p[:, 0:n], sgb_[:, 0:n], ALU.mult, [pb, sgbb], [m2b])
                c.tt("pool", mrg[:, e_, t0:t0 + n], m1[:, 0:n], m2[:, 0:n], ALU.add, [m1b, m2b], [mrgb[e_]])
        dump("mrg", mrg[:], [128, 8, NT], mrgb, bf=True)
        mg1.close()
        S.barrier()

        S.mark(f"L{l} outproj+LN1")
        r2 = Ring(c, "w2_", [128, 4, D], 2)
        r2.slots = [mgt(f"w2s{i}", [128, 4, D], BF16) for i in range(2)]

        def wo_view(t):
            return t[:].rearrange("p a b -> p (a b)").rearrange("p (k n) -> p k n", n=512)
        wo = []
        for hf in range(2):
            t_, b_ = r2.load(lambda t, hf=hf: [(wo_view(t), w_out[l][:, hf * 512:(hf + 1) * 512].rearrange("(k p) n -> p k n", p=128))])
            wo.append((wo_view(t_), b_))
        lnv, lnvb = mgt("lnv", [128, 2, D])
        load_lnv(lnv, lnvb, 0)
        gbc, gbcb = mgt("gbc", [128, D])
        vt, vtb = mgt("vt", [128, D])
        stats, statsb = mgt("stats", [128, 2, 6])
        mv, mvb = mgt("mv", [128, 4])
        for j in range(NTILE):
            seg = j // 2
            if j % 2 == 0:
                gbc_build(0, seg, gbc, gbcb)
            for hf in range(2):
                p, pb = c.ps()
                for k in range(8):
                    c.mm(p[:, :], mrg[:, k, j * 128:(j + 1) * 128], wo[hf][0][:, k, :], k == 0, k == 7, [mrgb[k], wo[hf][1]], [pb])
                c.tt("dve", vt[:, hf * 512:(hf + 1) * 512], p[:, :], gbc[:, hf * 512:(hf + 1) * 512], ALU.mult, [pb, gbcb], [vtb])
            c.stt("dve", vt[:], xt[:, j, :], ALPHA, vt[:], ALU.mult, ALU.add, [xb[j], vtb], [vtb])
            layer_norm_tile(j, vt, vtb, lnv, lnvb, stats, statsb, mv, mvb)
        dump("x1", xt[:], [128, NTILE, D], xb)
        mg_es.close()
        mx.close()
        S.barrier()

        S.mark(f"L{l} ffn")
        transpose_mod(1)
        ff = contextlib.ExitStack()

        def ft(name, shape, dt=F32, ff=ff, l=l):
            t = ff.enter_context(nc.sbuf_tensor(f"L{l}f_{name}", shape, dt))
            return t, Buf(f"L{l}f_{name}")
        facc = ff.enter_context(nc.sbuf_tensor(f"L{l}_facc", [128, NTILE, D], F32))
        faccb = [Buf(f"facc{j}") for j in range(NTILE)]
        c.memset("pool", facc[:], 0.0, faccb)
        r2 = Ring(c, "w2_", [128, 4, D], 2)
        r2.slots = [ft(f"w2s{i}", [128, 4, D], BF16) for i in range(2)]
        gT = [ft(f"gT{i}", [128, 4, NT], BF16) for i in range(2)]
        s1, s1b = ft("s1", [128, 512], BF16)
        comb, combb = ft("comb", [128, NTILE, NE])
        jdx = l // 2
        moe = (l % 2 == 1)
        if moe:
            rw, rwb = ft("rw", [128, 8, NE], BF16)
            c.dma("pool", rw[:], r_w[jdx].rearrange("(k p) n -> p k n", p=128), W=[rwb], sem="rw")
            rbr, rbrb = ft("rbr", [1, NE])
            c.dma("sp", rbr[:], r_b[jdx:jdx + 1, :], W=[rbrb])
            lgt, lgtb = ft("lgt", [128, NTILE, NE])
            p, pb = c.ps()
            for j in range(NTILE):
                o = p[:, j * NE:(j + 1) * NE]
                for k in range(8):
                    c.mm(o, hT[:, k, j * 128:(j + 1) * 128], rw[:, k, :], k == 0, False, [hTb, rwb], [pb])
                c.mm(o, ones32[0:1, 0:128], rbr[0:1, :], False, True, [ones32b, rbrb], [pb])
            c.cp("dve", lgt[:].rearrange("p j e -> p (j e)"), p[:, 0:NTILE * NE], [pb], [lgtb])
            m1_, m1_b = ft("rm1", [128, NTILE, 1])
            m2_, m2_b = ft("rm2", [128, NTILE, 1])
            tmpr, tmprb = ft("tmpr", [128, NTILE, NE])
            msk, mskb = ft("msk", [128, NTILE, NE])
            S.op("dve", lambda e: e.tensor_reduce(out=m1_[:].rearrange("p j o -> p (j o)"), in_=lgt[:], axis=AX.X, op=ALU.max), reads=[lgtb], writes=[m1_b])
            bshape = [128, NTILE, NE]
            c.tt("dve", msk[:], lgt[:], m1_[:].to_broadcast(bshape), ALU.is_equal, [lgtb, m1_b], [mskb])
            c.stt("dve", tmpr[:], msk[:], -1e30, lgt[:], ALU.mult, ALU.add, [mskb, lgtb], [tmprb])
            S.op("dve", lambda e: e.tensor_reduce(out=m2_[:].rearrange("p j o -> p (j o)"), in_=tmpr[:], axis=AX.X, op=ALU.max), reads=[tmprb], writes=[m2_b])
            c.tt("dve", msk[:], lgt[:], m2_[:].to_broadcast(bshape), ALU.is_ge, [lgtb, m2_b], [mskb])
            c.tt("dve", tmpr[:], lgt[:], m1_[:].to_broadcast(bshape), ALU.subtract, [lgtb, m1_b], [tmprb])
            c.act(tmpr[:], tmpr[:], AF.Exp, [tmprb], [tmprb])
            c.tt("dve", tmpr[:], tmpr[:], msk[:], ALU.mult, [tmprb, mskb], [tmprb])
            S.op("dve", lambda e: e.tensor_reduce(out=m2_[:].rearrange("p j o -> p (j o)"), in_=tmpr[:], axis=AX.X, op=ALU.add), reads=[tmprb], writes=[m2_b])
            S.op("dve", lambda e: e.reciprocal(out=m2_[:], in_=m2_[:]), reads=[m2_b], writes=[m2_b])
            c.tt("dve", comb[:], tmpr[:], m2_[:].to_broadcast(bshape), ALU.mult, [tmprb, m2_b], [combb])
            experts = [(moe_w1[jdx, e_], moe_w3[jdx, e_], moe_w2[jdx, e_], e_) for e_ in range(NE)]
        else:
            experts = [(ffn_w1[jdx], ffn_w3[jdx], ffn_w2[jdx], None)]

        fns13 = []
        fns2 = []
        for (w1, w3, w2, ei) in experts:
            for (f0, f1) in FGROUPS:
                for f in range(f0, f1):
                    fns13 += [w8fn(w1, f * 128), w8fn(w3, f * 128)]
                fns2.append(lambda t, w2=w2, f0=f0, f1=f1: [(t[:, 0:f1 - f0, :], w2[f0 * 128:f1 * 128, :].rearrange("(f p) n -> p f n", p=128))])
        pf13 = Prefetch(r8, fns13)
        pf2 = Prefetch(r2, fns2, depth=2)
        gi_ = 0
        for (w1, w3, w2, ei) in experts:
            for (f0, f1) in FGROUPS:
                g_, g_b = gT[gi_ % 2]
                gi_ += 1
                for fi, f in enumerate(range(f0, f1)):
                    w1t, w1b = pf13.next()
                    w3t, w3b = pf13.next()
                    for pi, (t0, n) in enumerate(PIECES):
                        p1, p1b = c.ps()
                        for k in range(8):
                            c.mm(p1[:, 0:n], w1t[:, k, :], hT[:, k, t0:t0 + n], k == 0, k == 7, [w1b, hTb], [p1b])
                        p3, p3b = c.ps()
                        for k in range(8):
                            c.mm(p3[:, 0:n], w3t[:, k, :], hT[:, k, t0:t0 + n], k == 0, k == 7, [w3b, hTb], [p3b])
                        c.act(s1[:, 0:n], p1[:, 0:n], AF.Silu, [p1b], [s1b])
                        c.tt("dve", g_[:, fi, t0:t0 + n], p3[:, 0:n], s1[:, 0:n], ALU.mult, [p3b, s1b], [g_b])
                w2t, w2b = pf2.next()
                for j in range(NTILE):
                    for hf in range(2):
                        p, pb = c.ps()
                        nf = f1 - f0
                        for fi in range(nf):
                            c.mm(p[:, :], g_[:, fi, j * 128:(j + 1) * 128], w2t[:, fi, hf * 512:(hf + 1) * 512], fi == 0, fi == nf - 1, [g_b, w2b], [pb])
                        sc = 1.0 if ei is None else comb[:, j, ei:ei + 1]
                        rd = [pb, faccb[j]] + ([] if ei is None else [combb])
                        c.stt("dve", facc[:, j, hf * 512:(hf + 1) * 512], p[:, :], sc, facc[:, j, hf * 512:(hf + 1) * 512], ALU.mult, ALU.add, rd, [faccb[j]])
        dump("facc", facc[:], [128, NTILE, D], faccb)
        S.mark(f"L{l} LN2")
        lnv, lnvb = ft("lnv2", [128, 2, D])
        load_lnv(lnv, lnvb, 1)
        gbc, gbcb = ft("gbc2", [128, D])
        stats, statsb = ft("stats2", [128, 2, 6])
        mv, mvb = ft("mv2", [128, 4])
        for j in range(NTILE):
            seg = j // 2
            if j % 2 == 0:
                gbc_build(1, seg, gbc, gbcb)
            c.tt("dve", facc[:, j, :], facc[:, j, :], gbc[:], ALU.mult, [faccb[j], gbcb], [faccb[j]])
            c.stt("dve", facc[:, j, :], xt[:, j, :], ALPHA, facc[:, j, :], ALU.mult, ALU.add, [xb[j], faccb[j]], [faccb[j]])
            layer_norm_tile(j, facc[:, j, :], faccb[j], lnv, lnvb, stats, statsb, mv, mvb)
        ff.close()
        es.close()
        S.barrier()

    c.dma("sp", yout.rearrange("(j p) d -> p j d", p=128), xt[:], R=xb, sem="yout", is_out=True)
    S.emit()
    c.st.close()
    print("n semaphores:", len(S.semnames))
    S.mark("end")
    nc._dbg_names = list(dbg.keys())
    nc._marks = S.marks
    return nc


def _consts():
    ident = np.eye(128, dtype=np.float32)
    s = np.arange(128)[:, None]
    t = np.arange(128)[None, :]
    same64 = (s // 64) == (t // 64)
    same32 = (s // 32) == (t // 32)
    amask = np.stack([(same32 & (s <= t)), (same32 & (s >= t)), (same64 & ((s // 32) < (t // 32))), (same64 & ((s // 32) > (t // 32)))], axis=1).astype(np.float32)
    bmask = np.stack([(s <= t), (s >= t)], axis=1).astype(np.float32)
    tt = np.arange(NT)
    smask = np.ascontiguousarray(np.broadcast_to((tt % 64 != 0), (128, NT))).astype(np.float32)
    sel = np.zeros((NSEG, NSEG, 128), np.float32)
    for k in range(NSEG):
        sel[k, k, :] = 1.0
    return ident, amask, bmask, smask, sel


def _cmask(is_sample):
    m = np.zeros((2, 128, 1154), np.float32)
    t = np.arange(1024)
    per = 64 if is_sample else 256
    m[0, :, PADA:PADA + 1024] = (t % per != per - 1)
    m[1, :, PADA:PADA + 1024] = (t % per != 0)
    return m


_NC_CACHE = {}


def kernel(_cores=None, _dbgout=None, **inp):
    inp = {k: np.ascontiguousarray(np.asarray(v)) for k, v in inp.items()}
    if "nc" not in _NC_CACHE:
        _NC_CACHE["nc"] = build()
    nc = _NC_CACHE["nc"]
    ident, amask, bmask, smask, sel = _consts()
    xp, xs = inp["x_prompt"], inp["x_sample"]
    wnames = ["w_ada", "b_ada", "w_in", "b_in", "hgrn_lb_raw", "hgrn_norm_g", "conv_w", "conv_b", "w_mq", "w_mk", "mlstm_fbias",
              "mlstm_norm_g", "w_branch_a", "w_branch_b", "w_out", "ln1_g", "ln1_b", "ln2_g", "ln2_b", "ffn_w1", "ffn_w3", "ffn_w2",
              "moe_router_w", "moe_router_b", "moe_w1", "moe_w3", "moe_w2"]
    in_maps = []
    segmap = []
    for cid in range(NCORES):
        if cid < 2:
            prompts = [cid]
            x = np.concatenate([xs[cid], xp[cid]], axis=0)
            cond = np.concatenate([np.repeat(inp["c"][cid:cid + 1], 4, axis=0), inp["c_ctx"][None]], axis=0)
            smp = True
            hinit = inp["state_hgrn"][cid]
            cinit = inp["state_mlstm_C"][cid]
            ninit = inp["state_mlstm_n"][cid]
            minit = inp["state_mlstm_m"][cid]
            segmap.append([None, None, None, None, cid])
        else:
            prompts = [2 + (cid - 2) * 5 + j for j in range(5)]
            x = np.concatenate([xp[p] for p in prompts], axis=0)
            cond = np.repeat(inp["c_ctx"][None], 5, axis=0)
            smp = False
            hinit = np.zeros((DEPTH, 2, 8, 128, 128), np.float32)
            cinit = np.zeros((DEPTH, 2, 8, 128, 128), np.float32)
            ninit = np.zeros((DEPTH, 2, 8, 128), np.float32)
            minit = np.zeros((DEPTH, 2, 8), np.float32)
            segmap.append(prompts)
        m = {"xin": np.ascontiguousarray(x), "cond": np.ascontiguousarray(cond),
             "flag": np.full((128, 1), 1.0 if smp else 0.0, np.float32), "cmask": _cmask(smp),
             "hinit": np.ascontiguousarray(hinit), "cinit": np.ascontiguousarray(cinit),
             "ninit": np.ascontiguousarray(ninit), "minit": np.ascontiguousarray(minit),
             "ident": ident, "amask": amask, "bmask": bmask, "smask": smask, "sel": sel}
        for w in wnames:
            m[w] = inp[w]
        in_maps.append(m)
    if _cores is not None:
        res = run_bass_kernel_spmd(nc, [in_maps[i] for i in _cores], core_ids=list(range(len(_cores))))
        if _dbgout is not None:
            for ci, cid in enumerate(_cores):
                _dbgout[cid] = res.results[ci]
        return None
    res = run_bass_kernel_spmd(nc, in_maps, core_ids=list(range(NCORES)))
    R = res.results
    y_prompt = np.zeros((32, 256, D), np.float32)
    y_sample = np.zeros((2, 1024, D), np.float32)
    nh = np.zeros((32, DEPTH, 2, 8, 128, 128), np.float32)
    nC = np.zeros((32, DEPTH, 2, 8, 128, 128), np.float32)
    nn = np.zeros((32, DEPTH, 2, 8, 128), np.float32)
    nm = np.zeros((32, DEPTH, 2, 8), np.float32)
    for cid in range(NCORES):
        r = R[cid]
        if cid < 2:
            y_sample[cid] = r["yout"][0:1024]
        for seg, p in enumerate(segmap[cid]):
            if p is None:
                continue
            y_prompt[p] = r["yout"][seg * 256:(seg + 1) * 256]
            nh[p] = r["o_h"][seg]
            nC[p] = r["o_C"][seg]
            nn[p] = r["o_n"][seg]
            nm[p] = r["o_m"][seg]
    return (y_prompt, y_sample, nh, nC, nn, nm)
```
